# Optimizing a Trainium2 kernel written in Bass

```python
import math
import jax, jax.numpy as jnp
from jax import lax
import numpy as np

D_MODEL = 2048
BATCH = 1
SEQ = 16384
DEPTH = 1

GDN_HEADS = 8
GDN_DK = 128
GDN_DV = 128
GDN_CONV = 4
GDN_CHUNK = 64
GDN_QK = GDN_HEADS * GDN_DK
GDN_V = GDN_HEADS * GDN_DV
GDN_CONV_CH = 2 * GDN_QK + GDN_V
SWA_HEADS = 16
SWA_KV_HEADS = 4
SWA_HEAD_DIM = 64
SWA_WINDOW = 128
SWA_BLOCK = 128
SWA_Q = SWA_HEADS * SWA_HEAD_DIM
SWA_KV = SWA_KV_HEADS * SWA_HEAD_DIM
REL_BUCKETS = 32
REL_MAX_DIST = 128
EPS = 1e-6
IN_SPLITS = (GDN_QK, GDN_QK, GDN_V, GDN_V, GDN_HEADS, GDN_HEADS,
             SWA_Q, SWA_KV, SWA_KV, SWA_Q, D_MODEL, D_MODEL)
IN_COLS = sum(IN_SPLITS)

kernel_name = "hybrid_gdn_swa_gated_merge_block"


def split_cols(a, sizes):
    offs = np.cumsum(sizes)[:-1].tolist()
    return jnp.split(a, offs, axis=-1)


def rms_norm(x, gain):
    xf = x.astype(jnp.float32)
    y = xf * lax.rsqrt(jnp.mean(xf * xf, axis=-1, keepdims=True) + EPS)
    return (y * gain.astype(jnp.float32)).astype(x.dtype)


def l2_norm(x):
    xf = x.astype(jnp.float32)
    return xf * lax.rsqrt(jnp.sum(xf * xf, axis=-1, keepdims=True) + EPS)


def causal_depthwise_conv(x, w):
    K, C = w.shape
    return lax.conv_general_dilated(
        x, w[:, None, :].astype(x.dtype), window_strides=(1,), padding=[(K - 1, 0)],
        dimension_numbers=("NWC", "WIO", "NWC"), feature_group_count=C)


def gated_delta_rule_chunked(q, k, v, g, beta):
    B, T, H, dk = q.shape
    dv = v.shape[-1]
    C = GDN_CHUNK
    N = T // C

    def chunk(a):
        return a.reshape(B, N, C, H, *a.shape[3:]).swapaxes(2, 3)

    q, k, v, beta = chunk(q), chunk(k), chunk(v), chunk(beta)
    g = jnp.cumsum(chunk(g), axis=-1)
    causal = jnp.tril(jnp.ones((C, C), bool))
    strict = jnp.tril(jnp.ones((C, C), bool), -1)
    gdiff = g[..., :, None] - g[..., None, :]
    decay = jnp.where(causal, jnp.exp(jnp.where(causal, gdiff, 0.0)), 0.0)
    k_beta = k * beta[..., None]
    v_beta = v * beta[..., None]
    L = jnp.where(strict, jnp.einsum("bnhid,bnhjd->bnhij", k_beta, k) * decay, 0.0)
    eye = jnp.eye(C, dtype=jnp.float32)
    t_inv = lax.linalg.triangular_solve(eye + L, jnp.broadcast_to(eye, L.shape),
                                        left_side=True, lower=True, unit_diagonal=True)
    u = jnp.einsum("bnhij,bnhje->bnhie", t_inv, v_beta)
    w = jnp.einsum("bnhij,bnhjd->bnhid", t_inv, k_beta * jnp.exp(g)[..., None])
    attn_intra = jnp.where(causal, jnp.einsum("bnhid,bnhjd->bnhij", q, k) * decay, 0.0)
    g_last = g[..., -1]
    q_dec = q * jnp.exp(g)[..., None]
    k_dec = k * jnp.exp(g_last[..., None] - g)[..., None]

    def step(S, inp):
        qd, kd, u_c, w_c, a_c, gl = inp
        v_new = u_c - jnp.einsum("bhcd,bhde->bhce", w_c, S)
        o = jnp.einsum("bhcd,bhde->bhce", qd, S) + jnp.einsum("bhij,bhje->bhie", a_c, v_new)
        S = S * jnp.exp(gl)[..., None, None] + jnp.einsum("bhcd,bhce->bhde", kd, v_new)
        return S, o

    xs = (jnp.moveaxis(q_dec, 1, 0), jnp.moveaxis(k_dec, 1, 0), jnp.moveaxis(u, 1, 0),
          jnp.moveaxis(w, 1, 0), jnp.moveaxis(attn_intra, 1, 0), jnp.moveaxis(g_last, 1, 0))
    S0 = jnp.zeros((B, H, dk, dv), jnp.float32)
    _, o = lax.scan(step, S0, xs)
    return jnp.moveaxis(o, 0, 1).swapaxes(2, 3).reshape(B, T, H, dv)


def t5_bucket(dist):
    max_exact = REL_BUCKETS // 2
    d = jnp.maximum(dist, 1).astype(jnp.float32)
    large = max_exact + (jnp.log(d / max_exact) / math.log(REL_MAX_DIST / max_exact)
                         * (REL_BUCKETS - max_exact)).astype(jnp.int32)
    large = jnp.minimum(large, REL_BUCKETS - 1)
    return jnp.where(dist < max_exact, dist, large)


def sliding_window_attention(q, k, v, sinks, rel_bias):
    B, T, Hq, d = q.shape
    Hkv = k.shape[2]
    G = Hq // Hkv
    Q = SWA_BLOCK
    N = T // Q
    qb = q.reshape(B, N, Q, Hkv, G, d)

    def band(a):
        a = a.reshape(B, N, Q, Hkv, d)
        prev = jnp.pad(a, ((0, 0), (1, 0), (0, 0), (0, 0), (0, 0)))[:, :-1]
        return jnp.concatenate([prev, a], axis=2)

    kb, vb = band(k), band(v)
    logits = jnp.einsum("bnqhgd,bnkhd->bnhgqk", qb, kb).astype(jnp.float32) * (d ** -0.5)
    qpos = jnp.arange(Q)[:, None] + Q
    kpos = jnp.arange(2 * Q)[None, :]
    dist = qpos - kpos
    in_window = (dist >= 0) & (dist < SWA_WINDOW)
    blk = jnp.arange(N)[:, None, None]
    valid = in_window[None] & ((blk * Q + kpos[None] - Q) >= 0)
    bias = rel_bias[t5_bucket(jnp.maximum(dist, 0))]
    bias = bias.transpose(2, 0, 1).reshape(Hkv, G, Q, 2 * Q).astype(jnp.float32)
    logits = jnp.where(valid[None, :, None, None], logits + bias[None, None], -jnp.inf)
    sink = sinks.astype(jnp.float32).reshape(Hkv, G)[None, None, :, :, None, None]
    m = jnp.maximum(jnp.max(logits, axis=-1, keepdims=True), sink)
    p = jnp.exp(logits - m)
    probs = (p / (jnp.sum(p, axis=-1, keepdims=True) + jnp.exp(sink - m))).astype(v.dtype)
    out = jnp.einsum("bnhgqk,bnkhd->bnqhgd", probs, vb)
    return out.reshape(B, T, Hq * d)


def setup_inputs(seed: int = 0) -> dict:
    key = jax.random.key(seed)
    ks = jax.random.split(key, 20)
    f32 = jnp.float32

    def normal(k, shape, scale):
        return jax.random.normal(k, shape, f32) * scale

    x = normal(ks[0], (BATCH, SEQ, D_MODEL), 1.0)
    c = normal(ks[1], (BATCH, D_MODEL), 1.0)
    w_ada = normal(ks[2], (DEPTH, D_MODEL, 3 * D_MODEL), 0.1 * D_MODEL ** -0.5)
    b_ada = normal(ks[3], (DEPTH, 3 * D_MODEL), 0.02)
    norm_gain = 1.0 + normal(ks[4], (DEPTH, D_MODEL), 0.02)
    w_in = normal(ks[5], (DEPTH, D_MODEL, IN_COLS), D_MODEL ** -0.5)
    conv_w = normal(ks[6], (DEPTH, GDN_CONV, GDN_CONV_CH), GDN_CONV ** -0.5)
    a_log = jnp.log(jax.random.uniform(ks[7], (DEPTH, GDN_HEADS), f32, 1.0, 16.0))
    dt = jnp.exp(jax.random.uniform(ks[8], (DEPTH, GDN_HEADS), f32, math.log(1e-3), math.log(1e-1)))
    dt_bias = dt + jnp.log(-jnp.expm1(-dt))
    gdn_norm_gain = 1.0 + normal(ks[9], (DEPTH, GDN_DV), 0.02)
    q_norm_gain = 1.0 + normal(ks[10], (DEPTH, SWA_HEAD_DIM), 0.02)
    k_norm_gain = 1.0 + normal(ks[11], (DEPTH, SWA_HEAD_DIM), 0.02)
    sinks = normal(ks[12], (DEPTH, SWA_HEADS), 0.5)
    rel_bias = normal(ks[13], (REL_BUCKETS, SWA_HEADS), 0.5)
    w_branch_gdn = normal(ks[14], (DEPTH, GDN_V, D_MODEL), GDN_V ** -0.5)
    w_branch_swa = normal(ks[15], (DEPTH, SWA_Q, D_MODEL), SWA_Q ** -0.5)
    w_out = normal(ks[16], (DEPTH, D_MODEL, D_MODEL), D_MODEL ** -0.5)
    return {"x": x, "c": c, "w_ada": w_ada, "b_ada": b_ada, "norm_gain": norm_gain,
            "w_in": w_in, "conv_w": conv_w, "a_log": a_log, "dt_bias": dt_bias,
            "gdn_norm_gain": gdn_norm_gain, "q_norm_gain": q_norm_gain,
            "k_norm_gain": k_norm_gain, "sinks": sinks, "rel_bias": rel_bias,
            "w_branch_gdn": w_branch_gdn, "w_branch_swa": w_branch_swa, "w_out": w_out}


def reference(x, c, w_ada, b_ada, norm_gain, w_in, conv_w, a_log, dt_bias, gdn_norm_gain,
              q_norm_gain, k_norm_gain, sinks, rel_bias, w_branch_gdn, w_branch_swa, w_out):
    B, T, _ = x.shape
    c_act = jax.nn.silu(c)
    for l in range(DEPTH):
        shift, scale, gate = jnp.split(c_act @ w_ada[l] + b_ada[l], 3, axis=-1)
        h = rms_norm(x, norm_gain[l]) * (1.0 + scale[:, None, :]) + shift[:, None, :]
        (q_a, k_a, v_a, z_a, b_a, a_a, q_b, k_b, v_b, z_b, gl_a, gl_b) = split_cols(h @ w_in[l], IN_SPLITS)

        qkv = jax.nn.silu(causal_depthwise_conv(jnp.concatenate([q_a, k_a, v_a], axis=-1), conv_w[l]))
        qa, ka, va = split_cols(qkv, (GDN_QK, GDN_QK, GDN_V))
        qa = l2_norm(qa.reshape(B, T, GDN_HEADS, GDN_DK)) * (GDN_DK ** -0.5)
        ka = l2_norm(ka.reshape(B, T, GDN_HEADS, GDN_DK))
        va = va.reshape(B, T, GDN_HEADS, GDN_DV).astype(jnp.float32)
        beta = jax.nn.sigmoid(b_a.astype(jnp.float32))
        g = -jnp.exp(a_log[l].astype(jnp.float32)) * jax.nn.softplus(
            a_a.astype(jnp.float32) + dt_bias[l].astype(jnp.float32))
        o_a = gated_delta_rule_chunked(qa, ka, va, g, beta)
        o_a = rms_norm(o_a, gdn_norm_gain[l]).reshape(B, T, GDN_V).astype(x.dtype) * jax.nn.silu(z_a)
        y_gdn = o_a @ w_branch_gdn[l]

        qb = rms_norm(q_b.reshape(B, T, SWA_HEADS, SWA_HEAD_DIM), q_norm_gain[l])
        kb = rms_norm(k_b.reshape(B, T, SWA_KV_HEADS, SWA_HEAD_DIM), k_norm_gain[l])
        vb = v_b.reshape(B, T, SWA_KV_HEADS, SWA_HEAD_DIM)
        o_b = sliding_window_attention(qb, kb, vb, sinks[l], rel_bias)
        y_swa = (o_b * jax.nn.silu(z_b)) @ w_branch_swa[l]

        mixed = jax.nn.sigmoid(gl_a) * y_gdn + jax.nn.sigmoid(gl_b) * y_swa
        x = x + gate[:, None, :] * (mixed @ w_out[l])
    return x
```

```python
import numpy as np
from contextlib import ExitStack
import concourse.bass as bass
import concourse.mybir as mybir
from concourse.bass_utils import run_bass_kernel_spmd

F32 = mybir.dt.float32
BF = mybir.dt.bfloat16
ALU = mybir.AluOpType
AF = mybir.ActivationFunctionType

D = 2048
EPS = 1e-6
NCORES = 8
TT = 256
CH = 64
BIG = 30000.0
SCHED = True
PRIO_BL = True
UNIFIED_BANKS = False
NB_T, NB_P = 1, 2
MODE = "replay"


class Buf:
    __slots__ = ("name", "last_w", "readers", "dma_sem", "dma_cnt")

    def __init__(self, name):
        self.name = name
        self.last_w = None
        self.readers = []
        self.dma_sem = None
        self.dma_cnt = 0


class Op:
    __slots__ = ("eng", "fn", "deps", "is_dma", "sem", "val", "signal", "idx", "cost", "busy", "inc")


class Prog:
    ENGS = ("pe", "act", "dve", "pool", "sp")

    def __init__(self, nc):
        self.nc = nc
        self.ops = []
        self.per_eng = {e: [] for e in self.ENGS}

    def op(self, eng, fn, reads=(), writes=(), dma=False, cost=0.3, busy=None, inc=16):
        o = Op()
        o.eng, o.fn, o.is_dma, o.signal, o.sem, o.val, o.idx = eng, fn, dma, False, None, 0, len(self.ops)
        o.cost = cost
        o.busy = cost if busy is None else busy
        deps = set()
        for b in reads:
            if b.last_w is not None:
                deps.add(b.last_w)
        for b in writes:
            if b.last_w is not None:
                deps.add(b.last_w)
            for r in b.readers:
                deps.add(r)
        deps.discard(o.idx)
        o.deps = sorted(deps)
        for b in reads:
            b.readers.append(o.idx)
        for b in writes:
            b.last_w = o.idx
            b.readers = []
        if dma:
            b = writes[0]
            o.sem = b
            b.dma_cnt += inc
            o.val = b.dma_cnt
            o.inc = inc
        self.ops.append(o)
        self.per_eng[eng].append(o)
        return o

    def schedule(self):
        ops = self.ops
        n = len(ops)
        indeg = [len(o.deps) for o in ops]
        succ = [[] for _ in range(n)]
        for o in ops:
            for d in o.deps:
                succ[d].append(o.idx)
        est = [0.0] * n
        fin = [0.0] * n
        bl = [0.0] * n
        if PRIO_BL:
            for i in range(n - 1, -1, -1):
                o = ops[i]
                m = 0.0
                for j in succ[i]:
                    if bl[j] > m:
                        m = bl[j]
                bl[i] = m + o.cost + 0.15
        cand = {e: [] for e in self.ENGS}
        for o in ops:
            if indeg[o.idx] == 0:
                cand[o.eng].append(o.idx)
        T = {e: 0.0 for e in self.ENGS}
        new = {e: [] for e in self.ENGS}
        done = 0
        WIN = 48
        while done < n:
            best = None
            for e in self.ENGS:
                cl = cand[e]
                if not cl:
                    continue
                te = T[e]
                for k in range(min(len(cl), WIN)):
                    i = cl[k]
                    st = est[i] if est[i] > te else te
                    key = (st, -bl[i], i)
                    if best is None or key < best[0]:
                        best = (key, e, k)
            (st, _p, i), e, k = best
            cand[e].pop(k)
            o = ops[i]
            T[e] = st + o.busy
            fin[i] = st + o.cost
            new[e].append(o)
            done += 1
            for j in succ[i]:
                if e == "pe" and ops[j].eng == "pe" and not o.is_dma and not ops[j].is_dma:
                    t2 = st + o.busy
                else:
                    t2 = fin[i] + (0.15 if ops[j].eng == e else 0.6)
                if t2 > est[j]:
                    est[j] = t2
                indeg[j] -= 1
                if indeg[j] == 0:
                    cl = cand[ops[j].eng]
                    lo, hi = 0, len(cl)
                    while lo < hi:
                        mid = (lo + hi) // 2
                        if cl[mid] < j:
                            lo = mid + 1
                        else:
                            hi = mid
                    cl.insert(lo, j)
        self.per_eng = new
        self.est_total = max(fin) if fin else 0.0

    def emit(self, stack):
        nc = self.nc
        ops = self.ops
        if SCHED:
            self.schedule()
        pos = {}
        for e in self.ENGS:
            for k, o in enumerate(self.per_eng[e]):
                pos[o.idx] = k
        for o in ops:
            latest = {}
            keep = []
            for d in o.deps:
                y = ops[d]
                if y.is_dma:
                    keep.append(d)
                    continue
                if y.eng == "pe" and o.eng == "pe" and not o.is_dma:
                    continue
                if y.eng not in latest or pos[d] > pos[latest[y.eng]]:
                    latest[y.eng] = d
            for d in latest.values():
                ops[d].signal = True
                keep.append(d)
            o.deps = keep
        esem = {}
        for e in ("pe", "act", "dve", "pool"):
            esem[e] = stack.enter_context(nc.semaphore("sem_" + e))
            n = 0
            for o in self.per_eng[e]:
                if o.is_dma:
                    continue
                if o.signal:
                    n += 1
                    o.sem, o.val = esem[e], n
        for o in ops:
            if o.is_dma:
                b = o.sem
                if b.dma_sem is None:
                    b.dma_sem = stack.enter_context(nc.semaphore("dsem_" + b.name))
                o.sem = b.dma_sem
        block = stack.enter_context(nc.Block())
        last_out = [o for o in ops if o.is_dma]

        def body(ename):
            def run(eng):
                waited = {}
                for o in self.per_eng[ename]:
                    for d in o.deps:
                        y = ops[d]
                        if (not y.is_dma) and y.eng == "pe" and ename == "pe" and not o.is_dma:
                            continue
                        key = id(y.sem)
                        if waited.get(key, 0) >= y.val:
                            continue
                        eng.wait_ge(y.sem, y.val)
                        waited[key] = y.val
                    ins = o.fn(eng)
                    if o.is_dma:
                        if o.inc == 16:
                            ins.then_inc(o.sem, 16)
                        else:
                            ins.then_inc(o.sem)
                    elif o.signal:
                        ins.then_inc(o.sem, 1)
                if ename == "sp":
                    fin = {}
                    for o in last_out:
                        fin[id(o.sem)] = (o.sem, max(o.val, fin.get(id(o.sem), (None, 0))[1]))
                    for sem, val in fin.values():
                        eng.wait_ge(sem, val)
            return run

        block.tensor(body("pe"))
        block.scalar(body("act"))
        block.vector(body("dve"))
        block.gpsimd(body("pool"))
        block.sync(body("sp"))


NCHK = TT // CH
NBLK = TT // 128


def build(NT, NOWN, debug=False, stop=99):
    nc = bass.Bass("TRN2", target_bir_lowering=False)
    NTOK = NT * TT
    stack = ExitStack()
    P = Prog(nc)
    FIRST_OWN = NT - NOWN

    def din(name, shape, dt=F32):
        return nc.dram_tensor(name, list(shape), dt, kind="ExternalInput").ap()

    def dout(name, shape, dt=F32):
        return nc.dram_tensor(name, list(shape), dt, kind="ExternalOutput").ap()

    x_d = din("x", [NTOK, D]); valid_d = din("valid", [128, NTOK]); validc_d = din("validc", [NTOK, 1])
    validm_d = din("validm", [NT * 64, NCHK])
    cT_d = din("cT", [128, 16]); wada_d = din("w_ada", [D, 3 * D]); badaP_d = din("badaP", [128, 48])
    bgate_d = din("bgate_rep", [128, D]); bada_rep_d = din("bada_rep", [128, 2 * D]); gainP_d = din("gainP", [128, 16])
    wk_d = din("wk", [D, 1024]); wv_d = din("wv", [D, 1024]); wbg_d = din("wbg", [D, 16])
    wq_d = din("wq", [D, 1024]); wz_d = din("wz", [D, 1024]); wqb_d = din("wqb", [D, 1024])
    wkb_d = din("wkb_dup", [D, 512]); wvb_d = din("wvb_dup", [D, 512]); wzb_d = din("wzb", [D, 1024])
    wgab_d = din("wgab", [D, 4096]); wbr_d = din("wbr", [1024, 4096]); wout_d = din("wout", [D, D])
    convw_d = din("convw", [128, 24, 4]); alog_d = din("alog_rep", [128, 8]); dtb_d = din("dtb_rep", [128, 8])
    ggain_d = din("gdn_gainP", [128, 1]); qg_d = din("qgain2", [128, 1]); kg_d = din("kgain2", [128, 1])
    sinks_d = din("sinks_rep", [128, 16]); biasT_d = din("biasT", [128, 16 * 2 * 128])
    identf_d = din("identf", [128, 128]); tri_d = din("tri", [128, 128]); blk64_d = din("blk64", [128, 128])
    maskL_d = din("maskL", [64, 64]); maskU_d = din("maskU", [64, 64])
    sel_d = din("sel_rep", [128, 8])
    sum_src = nc.dram_tensor("sum_src", [8 * 128, 8 * 256], F32)
    sum_dst = nc.dram_tensor("sum_dst", [8 * 128, 8 * 256], F32)
    out_d = dout("out", [NOWN * TT, D])
    dbg = {}
    if debug:
        dbg["hT"] = dout("dbg_hT", [128, 16 * TT], BF)
        dbg["knT"] = dout("dbg_knT", [128, 8 * TT], BF)
        dbg["vT"] = dout("dbg_vT", [128, 8 * TT], BF)
        dbg["S"] = dout("dbg_S", [128, 8 * 128])
        dbg["oT"] = dout("dbg_oT", [128, 8 * TT], BF)
        dbg["obT"] = dout("dbg_obT", [128, 8 * TT], BF)
        dbg["gc"] = dout("dbg_gc", [64, NCHK * 8])
        dbg["beta"] = dout("dbg_beta", [64, NCHK * 8])
        dbg["modP"] = dout("dbg_modP", [128, 32]); dbg["gate_row"] = dout("dbg_gate_row", [128, D], BF)
        dbg["cact"] = dout("dbg_cact", [128, 16], BF)

    def sb(name, shape, dt=F32):
        return nc.alloc_sbuf_tensor("s_" + name, list(shape), dt).ap()

    bufs = {}

    def B(name):
        if name not in bufs:
            bufs[name] = Buf(name)
        return bufs[name]

    identf = sb("identf", [128, 128]); identb = sb("identb", [128, 128], BF)
    onesf = sb("onesf", [128, 128]); onesb = sb("onesb", [128, 128], BF)
    tri = sb("tri", [128, 128]); blk64f = sb("blk64f", [128, 128]); blk64 = sb("blk64", [128, 128], BF)
    maskL = sb("maskL", [64, 64]); maskU = sb("maskU", [64, 64])
    convw = sb("convw", [128, 24, 4])
    alog = sb("alog", [128, 8]); dtb = sb("dtb", [128, 8]); negA = sb("negA", [128, 8])
    cT = sb("cT", [128, 16]); cact = sb("cact", [128, 16], BF); cactw = sb("cactw", [128, 16, 2], BF)
    badaP = sb("badaP", [128, 48]); modP = sb("modP", [128, 48]); gainP = sb("gainP", [128, 16])
    gammaP = sb("gammaP", [128, 16]); gate_row = sb("gate_row", [128, D], BF)
    ggain = sb("ggain", [128, 1]); qg8 = sb("qg8", [128, 1]); kg8 = sb("kg8", [128, 1])
    esink = sb("esink", [128, 16])
    wk = sb("wk", [128, 16, 1024], BF); wv = sb("wv", [128, 16, 1024], BF); wbg = sb("wbg", [128, 16, 16], BF)
    wst = [sb("wst%d" % i, [128, 16, 256], BF) for i in range(2)]
    wbb0_ = sb("wbb0", [128, 8, 256], BF)
    xs = [sb("xs%d" % i, [128, D]) for i in range(2)]
    ss = sb("ss", [128, 2]); rstd = sb("rstd", [128, 2])
    xn = sb("xn", [128, NBLK, D], BF)
    mixedT = xn.rearrange("p a b -> p (a b)").rearrange("p (k t) -> p k t", t=TT)
    hT = sb("hT", [128, 16, TT], BF)
    validt = sb("validt", [128, TT])
    halo = sb("halo", [128, 24, 3])
    pre = [sb("pre%d" % i, [128, TT + 3]) for i in range(2)]
    caccs = [sb("cacc%d" % i, [128, TT]) for i in range(2)]; cact2s = [sb("cact2_%d" % i, [128, TT]) for i in range(2)]
    sqks = [sb("sqk%d" % i, [128, TT], BF) for i in range(2)]; rinvs = [sb("rinv%d" % i, [128, TT]) for i in range(2)]
    cacc, cact2, sqk, rinv = caccs[0], cact2s[0], sqks[0], rinvs[0]
    kvT = sb("kvT", [128, 16, TT], BF); knT = kvT[:, 0:8, :]; vT = kvT[:, 8:16, :]
    wbb = [wbb0_, kvT[:, 0:8, :]]
    wbb_bufs = [[B("wbb0")], [B("knT_%d" % h) for h in range(8)]]
    SCB = {}
    for _nm, _shp in (("sc_raw", [64, NCHK, 16]), ("beta", [64, NCHK, 8]), ("gtok", [64, NCHK, 8]), ("etmp", [64, NCHK, 8]),
                      ("gc", [64, NCHK, 8]), ("glb", [128, NCHK, 8]), ("eglb", [128, NCHK, 8]), ("c_bg", [64, NCHK, 8]),
                      ("c_kd", [64, NCHK, 8]), ("c_eg", [64, NCHK, 8]), ("negb", [64, NCHK, 8]), ("tmp88", [64, NCHK, 8]),
                      ("vcm", [64, NCHK])):
        SCB[_nm] = [sb("%s_p%d" % (_nm, i), _shp) for i in range(2)]
    cur_tp = [0]

    def SC(nm):
        return SCB[nm][cur_tp[0]]

    def BS(nm):
        return B("%s_p%d" % (nm, cur_tp[0]))
    NHB = 3
    dghs = [sb("dgh0", [64, NCHK, 64])] * NHB
    args = [sb("arg%d" % i, [64, NCHK, 64]) for i in range(NHB)]; Dms = [sb("Dm%d" % i, [64, NCHK, 64]) for i in range(NHB)]
    Nms = [sb("Nm%d" % i, [64, NCHK, 64], BF) for i in range(NHB)]; Pms = [sb("Pm%d" % i, [64, NCHK, 64], BF) for i in range(NHB)]
    Qms = [sb("Qm%d" % i, [64, NCHK, 64], BF) for i in range(NHB)]; Rms = [sb("Rm%d" % i, [64, NCHK, 64], BF) for i in range(NHB)]
    Rfs = [None] * NHB
    kbgs = [sb("kbg0", [64, NCHK, 128], BF)] * NHB; vbms = [sb("vbm0", [64, NCHK, 128], BF)] * NHB
    kdm = [sb("kdm%d" % h, [64, NCHK, 128], BF) for h in range(4)] * 2
    Um = [sb("Um%d" % h, [64, NCHK, 128], BF) for h in range(4)] * 2
    WT = [sb("WT%d" % h, [128, NCHK, 64], BF) for h in range(4)] * 2
    vnew = [sb("vnew%d" % h, [64, 128], BF) for h in range(4)] * 2
    S32 = sb("S32", [128, 8, 128]); Sb = sb("Sb", [128, 8, 128], BF)
    vnewM = [sb("vnewM%d" % h, [64, 128], BF) for h in range(4)] * 2
    selr = sb("selr", [128, 8])
    qnT = sb("qnT", [128, 8, TT], BF); zT = sb("zT", [128, 8, TT], BF)
    AT = [sb("AT%d" % h, [64, NCHK, 64], BF) for h in range(4)] * 2
    oT = sb("oT", [128, 8, TT], BF); obT = qnT
    q2T64 = kvT[0:64, :, :]; zbT = zT
    kbT = sb("kbT", [128, 4, 128 + TT], BF); vtok = sb("vtok", [128, 1 + NBLK, 512], BF)
    keyneg = sb("keyneg", [128, 1 + NBLK])
    biasb = [sb("biasb0", [128, 512])] * 2
    tbuf = sb("tbuf", [128, 512]); pTb = sb("pTb", [128, 512], BF)
    rden = sb("rden", [128, 128]); otmp = sb("otmp", [128, 128])
    o1 = [sb("o1_%d" % i, [64, 128]) for i in range(2)]
    ojunk = sb("ojunk", [64, 128], BF); oss = sb("oss", [64, 2]); onb = [sb("onb%d" % i, [64, 128], BF) for i in range(2)]
    sga = cacc; sgb = cact2; mt1 = rinv; mt2 = pre[0][:, 0:TT]
    res = [tbuf[:, 0:256], tbuf[:, 256:512]]

    pbank = [nc.alloc_psum_tensor("pb%d" % i, [128, 512], F32).ap() for i in range(8)]
    pbuf = [B("pb%d" % i) for i in range(8)]
    rr = {"g": 0, "p": 0, "t": 0}

    def bank(kind):
        if UNIFIED_BANKS:
            i = rr["g"]; rr["g"] = (i + 1) % 8; return i
        if kind == "t" and NB_T > 0:
            i = rr["t"]; rr["t"] = (i + 1) % NB_T; return i
        if kind == "p" and NB_P > 0:
            i = NB_T + rr["p"]; rr["p"] = (rr["p"] + 1) % NB_P; return i
        i = NB_T + NB_P + rr["g"]; rr["g"] = (rr["g"] + 1) % (8 - NB_T - NB_P); return i

    def nfree(ap):
        n = 1
        for d_ in ap.shape[1:]:
            n *= d_
        return n

    def dma(eng, out, in_, reads, writes):
        nb = nfree(out) * out.shape[0] * 4
        P.op(eng, lambda e: e.dma_start(out=out, in_=in_), reads=reads, writes=writes, dma=True,
             cost=2.0 + nb / 180e3, busy=0.15 if eng == "sp" else 1.0)

    def act(out, in_, func, reads, writes, bias=None, scale=None, accum=None):
        kw = {}
        if bias is not None: kw["bias"] = bias
        if scale is not None: kw["scale"] = scale
        if accum is not None: kw["accum_out"] = accum
        P.op("act", lambda e: e.activation(out, in_, func, **kw), reads=reads, writes=writes, cost=0.22 + nfree(out) / 1100.0)

    def vcost(eng, out):
        return (0.08 + nfree(out) / 900.0) if eng == "dve" else (0.15 + nfree(out) / 450.0)

    def ts(eng, out, in0, s1, s2, op0, op1, reads, writes):
        if op1 is None:
            P.op(eng, lambda e: e.tensor_scalar(out, in0, s1, 0.0, op0, ALU.add), reads=reads, writes=writes, cost=vcost(eng, out))
        else:
            P.op(eng, lambda e: e.tensor_scalar(out, in0, s1, s2, op0, op1), reads=reads, writes=writes, cost=vcost(eng, out))

    def recip(out, in_, reads, writes):
        P.op("dve", lambda e: e.reciprocal(out, in_), reads=reads, writes=writes, cost=vcost("dve", out))

    def rsqrt(out, in0, s_mul, s_add, reads, writes):
        ts("dve", out, in0, s_mul, s_add, ALU.mult, ALU.add, reads, writes)
        act(out, out, AF.Ln, writes, writes)
        act(out, out, AF.Exp, writes, writes, scale=-0.5)

    def tt(eng, out, in0, in1, op, reads, writes):
        P.op(eng, lambda e: e.tensor_tensor(out, in0, in1, op), reads=reads, writes=writes, cost=vcost(eng, out))

    def stt(eng, out, in0, scalar, in1, op0, op1, reads, writes):
        P.op(eng, lambda e: e.scalar_tensor_tensor(out, in0, scalar, in1, op0, op1), reads=reads, writes=writes, cost=vcost(eng, out))

    def cp(eng, out, in_, reads, writes):
        if eng == "act":
            P.op(eng, lambda e: e.activation(out, in_, AF.Copy), reads=reads, writes=writes, cost=0.22 + nfree(out) / 1100.0)
        else:
            P.op(eng, lambda e: e.tensor_copy(out, in_), reads=reads, writes=writes, cost=vcost(eng, out))

    def mm(out, lhsT, rhs, start, stop, reads, writes):
        npass = 4 if "float32" in str(rhs.dtype) else 1
        P.op("pe", lambda e: e.matmul(out, lhsT, rhs, start=start, stop=stop), reads=reads, writes=writes,
             cost=0.25 + nfree(rhs) * npass / 2400.0, busy=0.035 + nfree(rhs) * npass / 2400.0)

    def tr(out, in_, ident, reads, writes):
        P.op("pe", lambda e: e.transpose(out, in_, ident), reads=reads, writes=writes, cost=0.3, busy=0.07)

    def memset(eng, ap, val, writes):
        P.op(eng, lambda e: e.memset(ap, val), reads=[], writes=writes, cost=vcost(eng, ap))

    wst_i = [0]

    def wstream(src_d, c0, rows=D):
        i = wst_i[0] % 2; wst_i[0] += 1
        src = src_d[:, c0:c0 + 256].rearrange("(k p) c -> p k c", p=128)
        bl = []
        for k0 in range(0, 16, 8):
            b = B("wst%d_%d" % (i, k0)); bl.append(b)
            dma("pool", wst[i][:, k0:k0 + 8, :], src[:, k0:k0 + 8, :], [], [b])
        return wst[i], [bl[kc // 8] for kc in range(16)]

    for (dst, src, nm) in [(identf, identf_d, "identf"), (tri, tri_d, "tri"), (blk64f, blk64_d, "blk64f"),
                           (maskL, maskL_d, "maskL"), (maskU, maskU_d, "maskU"), (convw, convw_d, "convw"),
                           (alog, alog_d, "alog"), (dtb, dtb_d, "dtb"), (cT, cT_d, "cT"), (badaP, badaP_d, "badaP"),
                           (gainP, gainP_d, "gainP"), (ggain, ggain_d, "ggain"),
                           (qg8, qg_d, "qg8"), (kg8, kg_d, "kg8"), (esink, sinks_d, "esink")] + ([(selr, sel_d, "selr")] if MODE == "cc" else []):
        dma("sp", dst, src, [], [B(nm)])
    dma("pool", gate_row, bgate_d, [], [B("gate_row")])
    cp("dve", identb, identf, [B("identf")], [B("identb")])
    cp("dve", blk64, blk64f, [B("blk64f")], [B("blk64")])
    memset("pool", onesf, 1.0, [B("onesf")]); memset("pool", onesb, 1.0, [B("onesb")])
    memset("pool", halo, 0.0, [B("halo%d" % c) for c in range(24)])
    memset("pool", S32, 0.0, [B("S32_%d" % h) for h in range(8)])
    memset("pool", Sb, 0.0, [B("Sb_%d" % h) for h in range(8)])
    memset("pool", kbT, 0.0, [B("kbT")]); memset("pool", vtok, 0.0, [B("vtok")]); memset("pool", keyneg, -BIG, [B("keyneg")])
    for (dst, src, nm) in [(wk, wk_d, "wk"), (wv, wv_d, "wv")]:
        s_ = src.rearrange("(k p) c -> p k c", p=128)
        for k0 in range(0, 16, 4):
            dma("pool", dst[:, k0:k0 + 4, :], s_[:, k0:k0 + 4, :], [], [B(nm + "_%d" % k0)])
    dma("pool", wbg, wbg_d.rearrange("(k p) c -> p k c", p=128), [], [B("wbg")])
    act(negA, alog, AF.Exp, [B("alog")], [B("negA")])
    ts("dve", negA, negA, -1.0, None, ALU.mult, None, [B("negA")], [B("negA")])
    ts("dve", qg8, qg8, 8.0, None, ALU.mult, None, [B("qg8")], [B("qg8")])
    ts("dve", kg8, kg8, 8.0, None, ALU.mult, None, [B("kg8")], [B("kg8")])
    act(esink, esink, AF.Exp, [B("esink")], [B("esink")])

    act(cact, cT, AF.Silu, [B("cT")], [B("cact")])
    cactrep = xn[:, 0, :].rearrange("p (k m) -> p k m", m=128)
    cp("dve", cactrep, cact.unsqueeze(2).broadcast_to([128, 16, 128]), [B("cact")], [B("xn0")])
    cp("dve", cactw, cact.unsqueeze(2).broadcast_to([128, 16, 2]), [B("cact")], [B("cactw")])
    adab = tbuf[:, 0:256]; adat = otmp
    for blk in range(24):
        wb, wbl = wstream(wada_d, blk * 256)
        pg = bank("g")
        for kc in range(16):
            mm(pbank[pg][:, 0:256], cactrep[:, kc, :], wb[:, kc, :], kc == 0, kc == 15,
               [wbl[kc], B("xn0")], [pbuf[pg]])
        if blk >= 16:
            c0 = (blk - 16) * 256
            tt("dve", gate_row[:, c0:c0 + 256], pbank[pg][:, 0:256], gate_row[:, c0:c0 + 256], ALU.add,
               [pbuf[pg], B("gate_row")], [B("gate_row")])
        else:
            dma("sp", adab, bada_rep_d[:, blk * 256:(blk + 1) * 256], [], [B("tbuf")])
            tt("dve", adab, pbank[pg][:, 0:256], adab, ALU.add, [pbuf[pg], B("tbuf")], [B("tbuf")])
            for j in range(2):
                col = blk * 2 + j
                tt("dve", adat, adab[:, j * 128:(j + 1) * 128], identf, ALU.mult, [B("tbuf"), B("identf")], [B("otmp")])
                P.op("dve", lambda e, o=modP[:, col:col + 1]: e.tensor_reduce(o, adat, mybir.AxisListType.X, ALU.add),
                     reads=[B("otmp")], writes=[B("modP")])
    stt("dve", gammaP, modP[:, 16:32], 1.0, gainP, ALU.add, ALU.mult, [B("modP"), B("gainP")], [B("gammaP")])

    hbs = [B("hT_%d" % kc) for kc in range(16)]

    def load_h(t):
        for s in range(NBLK):
            r0 = t * TT + s * 128
            dma("sp", xs[s], x_d[r0:r0 + 128, :], [], [B("xs%d" % s)])
            memset("pool", ss[:, s:s + 1], 0.0, [B("ss%d" % s)])
            act(xn[:, s, :], xs[s], AF.Square, [B("xs%d" % s), B("ss%d" % s)], [B("xn%d" % s), B("ss%d" % s)],
                accum=ss[:, s:s + 1])
            rsqrt(rstd[:, s:s + 1], ss[:, s:s + 1], 1.0 / D, EPS, [B("ss%d" % s)], [B("rstd%d" % s)])
            ts("dve", xn[:, s, :], xs[s], rstd[:, s:s + 1], None, ALU.mult, None,
               [B("xs%d" % s), B("rstd%d" % s)], [B("xn%d" % s)])
        for kc in range(16):
            tb = bank("t")
            pT = pbank[tb].bitcast(BF)
            for s in range(NBLK):
                tr(pT[:, s * 128:(s + 1) * 128], xn[:, s, kc * 128:(kc + 1) * 128], identb,
                   [B("xn%d" % s), B("identb")], [pbuf[tb]])
            if kc % 2 == 0:
                act(hT[:, kc, :], pT[:, 0:TT], AF.Identity, [pbuf[tb], B("gammaP"), B("modP")], [hbs[kc]],
                    bias=modP[:, kc:kc + 1], scale=gammaP[:, kc:kc + 1])
            else:
                ts("dve", hT[:, kc, :], pT[:, 0:TT], gammaP[:, kc:kc + 1], modP[:, kc:kc + 1], ALU.mult, ALU.add,
                   [pbuf[tb], B("gammaP"), B("modP")], [hbs[kc]])
        dma("sp", validt, valid_d[:, t * TT:(t + 1) * TT], [], [B("validt")])

    cc_count = [0]

    def conv_chunk(pb, cc, dst, dst_bufs, l2, scale_q=None):
        i = cc_count[0] % 2; cc_count[0] += 1
        pr = pre[i]; bp = B("pre%d" % i)
        cacc, cact2, sqk, rinv = caccs[i], cact2s[i], sqks[i], rinvs[i]
        ba = B("cacc" if i == 0 else "cacc1"); b2 = B("cact2" if i == 0 else "cact2_1")
        bsq = B("sqk" if i == 0 else "sqk1"); bri = B("rinv" if i == 0 else "rinv1")
        cp("pool", pr[:, 0:3], halo[:, cc, :], [B("halo%d" % cc)], [bp])
        cp("act", pr[:, 3:TT + 3], pbank[pb][:, 0:TT], [pbuf[pb], bp], [bp])
        tt("pool", halo[:, cc, :], pr[:, TT:TT + 3], validt[:, TT - 3:TT], ALU.mult, [bp, B("validt")], [B("halo%d" % cc)])
        ts("dve", cacc, pr[:, 0:TT], convw[:, cc, 0:1], None, ALU.mult, None, [bp, B("convw")], [ba])
        stt("dve", cacc, pr[:, 1:TT + 1], convw[:, cc, 1:2], cacc, ALU.mult, ALU.add, [bp, ba, B("convw")], [ba])
        stt("dve", cacc, pr[:, 2:TT + 2], convw[:, cc, 2:3], cacc, ALU.mult, ALU.add, [bp, ba, B("convw")], [ba])
        stt("dve", cacc, pr[:, 3:TT + 3], convw[:, cc, 3:4], cacc, ALU.mult, ALU.add, [bp, ba, B("convw")], [ba])
        if not l2:
            act(dst, cacc, AF.Silu, [ba], dst_bufs)
            return
        act(cact2, cacc, AF.Silu, [ba], [b2])
        act(sqk, cact2, AF.Square, [b2], [bsq])
        g = bank("g")
        mm(pbank[g][:, 0:TT], onesb, sqk, True, True, [B("onesb"), bsq], [pbuf[g]])
        rsqrt(rinv, pbank[g][:, 0:TT], 1.0, EPS, [pbuf[g]], [bri])
        if scale_q is None:
            tt("pool", dst, cact2, rinv, ALU.mult, [b2, bri], dst_bufs)
        else:
            stt("dve", dst, cact2, scale_q, rinv, ALU.mult, ALU.mult, [b2, bri], dst_bufs)

    def proj_fm(w, wbl, c0):
        pb = bank("p")
        for kc in range(16):
            mm(pbank[pb][:, 0:TT], w[:, kc, c0:c0 + 128], hT[:, kc, :], kc == 0, kc == 15, [wbl[kc], hbs[kc]], [pbuf[pb]])
        return pb

    def rmsnorm64(pb, gcol, gbuf, dst, dst_bufs):
        act(sqk, pbank[pb][:, 0:TT], AF.Square, [pbuf[pb]], [B("sqk")])
        g = bank("g")
        mm(pbank[g][:, 0:TT], blk64, sqk, True, True, [B("blk64"), B("sqk")], [pbuf[g]])
        rsqrt(rinv, pbank[g][:, 0:TT], 1.0, 64.0 * EPS, [pbuf[g]], [B("rinv")])
        stt("dve", dst, pbank[pb][:, 0:TT], gcol, rinv, ALU.mult, ALU.mult, [pbuf[pb], B("rinv"), gbuf], dst_bufs)

    wkbl = [B("wk_%d" % (kc // 4 * 4)) for kc in range(16)]
    wvbl = [B("wv_%d" % (kc // 4 * 4)) for kc in range(16)]

    def scalars(t):
        sc_raw = SC("sc_raw"); beta = SC("beta"); gtok = SC("gtok"); etmp = SC("etmp"); gc = SC("gc"); glb = SC("glb"); eglb = SC("eglb"); c_bg = SC("c_bg"); c_kd = SC("c_kd"); c_eg = SC("c_eg"); negb = SC("negb"); tmp88 = SC("tmp88"); vcm = SC("vcm")
        g0 = bank("g")
        for c in range(NCHK):
            for kc in range(16):
                mm(pbank[g0][0:64, c * 16:(c + 1) * 16], hT[:, kc, c * 64:(c + 1) * 64], wbg[:, kc, :],
                   kc == 0, kc == 15, [hbs[kc], B("wbg")], [pbuf[g0]])
        scb = BS("sc_raw")
        cp("dve", sc_raw, pbank[g0][0:64, 0:NCHK * 16].rearrange("p (c k) -> p c k", k=16), [pbuf[g0]], [scb])
        act(etmp, sc_raw[:, :, 0:8], AF.Exp, [scb], [BS("etmp")], scale=-1.0)
        ts("dve", etmp, etmp, 1.0, None, ALU.add, None, [BS("etmp")], [BS("etmp")])
        recip(etmp, etmp, [BS("etmp")], [BS("etmp")])
        dma("sp", vcm, validm_d[t * 64:(t + 1) * 64, :], [], [BS("vcm")])
        tt("dve", beta, etmp, vcm.unsqueeze(2).broadcast_to([64, NCHK, 8]), ALU.mult, [BS("etmp"), BS("vcm")], [BS("beta")])
        tt("dve", gtok, sc_raw[:, :, 8:16], dtb[0:64, :].unsqueeze(1).broadcast_to([64, NCHK, 8]), ALU.add,
           [scb, B("dtb")], [BS("gtok")])
        act(gtok, gtok, AF.Exp, [BS("gtok")], [BS("gtok")])
        ts("dve", gtok, gtok, 1.0, None, ALU.add, None, [BS("gtok")], [BS("gtok")])
        act(gtok, gtok, AF.Ln, [BS("gtok")], [BS("gtok")])
        tt("dve", gtok, gtok, negA[0:64, :].unsqueeze(1).broadcast_to([64, NCHK, 8]), ALU.mult,
           [BS("gtok"), B("negA")], [BS("gtok")])
        g1 = bank("g")
        gflat = gtok.rearrange("p c h -> p (c h)")
        NW = NCHK * 8
        mm(pbank[g1][0:64, 0:NW], tri[0:64, 0:64], gflat, True, True, [B("tri"), BS("gtok")], [pbuf[g1]])
        mm(pbank[g1][:, 64:64 + NW], onesf[0:64, :], gflat, True, True, [B("onesf"), BS("gtok")], [pbuf[g1]])
        cp("dve", gc.rearrange("p c h -> p (c h)"), pbank[g1][0:64, 0:NW], [pbuf[g1]], [BS("gc")])
        cp("dve", glb.rearrange("p c h -> p (c h)"), pbank[g1][:, 64:64 + NW], [pbuf[g1]], [BS("glb")])
        act(eglb, glb, AF.Exp, [BS("glb")], [BS("eglb")])
        act(c_eg, gc, AF.Exp, [BS("gc")], [BS("c_eg")])
        tt("dve", c_bg, c_eg, beta, ALU.mult, [BS("c_eg"), BS("beta")], [BS("c_bg")])
        tt("dve", tmp88, glb[0:64], gc, ALU.subtract, [BS("glb"), BS("gc")], [BS("tmp88")])
        act(c_kd, tmp88, AF.Exp, [BS("tmp88")], [BS("c_kd")])
        ts("dve", negb, beta, -1.0, None, ALU.mult, None, [BS("beta")], [BS("negb")])

    def vw(ap, n=64):
        return ap.rearrange("p (c j) -> p c j", j=n)

    def head_state(h, own):
        sc_raw = SC("sc_raw"); beta = SC("beta"); gtok = SC("gtok"); etmp = SC("etmp"); gc = SC("gc"); glb = SC("glb"); eglb = SC("eglb"); c_bg = SC("c_bg"); c_kd = SC("c_kd"); c_eg = SC("c_eg"); negb = SC("negb"); tmp88 = SC("tmp88"); vcm = SC("vcm")
        hi = h % NHB
        dgh, arg, Dm, Nm, Pm, Qm, Rm, Rf, kbg, vbm = (dghs[hi], args[hi], Dms[hi], Nms[hi], Pms[hi], Qms[hi], Rms[hi],
                                                    Rfs[hi], kbgs[hi], vbms[hi])
        sfx = "_%d" % hi
        kn_b = B("knT_%d" % h); v_b = B("vT_%d" % h)
        tt("dve", dgh, identf[0:64, 0:64].unsqueeze(1).broadcast_to([64, NCHK, 64]),
           gc[:, :, h].unsqueeze(2).broadcast_to([64, NCHK, 64]), ALU.mult, [B("identf"), BS("gc")], [B("dgh")])
        g0 = bank("g")
        mm(pbank[g0][0:64, 0:NCHK * 64], onesf[0:64, 0:64], dgh.rearrange("p c j -> p (c j)"), True, True,
           [B("onesf"), B("dgh")], [pbuf[g0]])
        gcb = gc[:, :, h].unsqueeze(2).broadcast_to([64, NCHK, 64])
        tt("dve", arg, vw(pbank[g0][0:64, 0:NCHK * 64]), gcb, ALU.subtract, [pbuf[g0], BS("gc")], [B("arg" + sfx)])
        if own:
            stt("dve", Dm, arg, 0.0, maskU.unsqueeze(1).broadcast_to([64, NCHK, 64]), ALU.min, ALU.add,
                [B("arg" + sfx), B("maskU")], [B("Dm" + sfx)])
            act(Dm, Dm, AF.Exp, [B("Dm" + sfx)], [B("Dm" + sfx)])
            g2 = bank("g")
            for c in range(NCHK):
                sl = slice(c * 64, (c + 1) * 64)
                mm(pbank[g2][0:64, sl], knT[:, h, sl], qnT[:, h, sl], True, True, [kn_b, B("qnT_%d" % h)], [pbuf[g2]])
            tt("dve", AT[h], vw(pbank[g2][0:64, 0:NCHK * 64]), Dm, ALU.mult, [pbuf[g2], B("Dm" + sfx)], [B("AT%d" % (h % 4))])
        stt("dve", arg, arg, 0.0, maskL.unsqueeze(1).broadcast_to([64, NCHK, 64]), ALU.max, ALU.add,
            [B("arg" + sfx), B("maskL")], [B("arg" + sfx)])
        act(Dm, arg, AF.Exp, [B("arg" + sfx)], [B("Dm" + sfx)], scale=-1.0)
        g1 = bank("g")
        for c in range(NCHK):
            sl = slice(c * 64, (c + 1) * 64)
            mm(pbank[g1][0:64, sl], knT[:, h, sl], knT[:, h, sl], True, True, [kn_b], [pbuf[g1]])
        tt("dve", Dm, Dm, negb[:, :, h].unsqueeze(2).broadcast_to([64, NCHK, 64]), ALU.mult, [B("Dm" + sfx), BS("negb")], [B("Dm" + sfx)])
        tt("dve", Nm, vw(pbank[g1][0:64, 0:NCHK * 64]), Dm, ALU.mult, [pbuf[g1], B("Dm" + sfx)], [B("Nm" + sfx)])
        tb = bank("t"); pT = pbank[tb].bitcast(BF)
        for c in range(NCHK):
            tr(pT[0:64, c * 64:(c + 1) * 64], Nm[:, c, :], identb[0:64, 0:64], [B("Nm" + sfx), B("identb")], [pbuf[tb]])
        cp("act", Pm, vw(pT[0:64, 0:NCHK * 64]), [pbuf[tb]], [B("Pm" + sfx)])
        tt("dve", Rm, Pm, identf[0:64, 0:64].unsqueeze(1).broadcast_to([64, NCHK, 64]), ALU.add,
           [B("Pm" + sfx), B("identf")], [B("Rm" + sfx)])
        Qc, Qn = Nm, "Nm" + sfx
        for lvl in range(5):
            gp = bank("g"); gq = bank("g")
            if lvl < 4:
                for c in range(NCHK):
                    sl = slice(c * 64, (c + 1) * 64)
                    mm(pbank[gp][0:64, sl], Qc[:, c, :], Pm[:, c, :], True, True, [B(Qn), B("Pm" + sfx)], [pbuf[gp]])
            for c in range(NCHK):
                sl = slice(c * 64, (c + 1) * 64)
                mm(pbank[gq][0:64, sl], Pm[:, c, :], Qc[:, c, :], True, True, [B(Qn), B("Pm" + sfx)], [pbuf[gq]])
            if lvl < 4:
                cp("act", Pm, vw(pbank[gp][0:64, 0:NCHK * 64]), [pbuf[gp]], [B("Pm" + sfx)])
            cp("dve" if lvl % 2 == 0 else "act", Qm, vw(pbank[gq][0:64, 0:NCHK * 64]), [pbuf[gq]], [B("Qm" + sfx)])
            Qc, Qn = Qm, "Qm" + sfx
            gr = bank("g")
            for c in range(NCHK):
                sl = slice(c * 64, (c + 1) * 64)
                mm(pbank[gr][0:64, sl], Qm[:, c, :], Rm[:, c, :], True, True, [B("Qm" + sfx), B("Rm" + sfx)], [pbuf[gr]])
            tt("dve", Rm, vw(pbank[gr][0:64, 0:NCHK * 64]), Rm, ALU.add, [pbuf[gr], B("Rm" + sfx)], [B("Rm" + sfx)])
        tb = bank("t"); pT = pbank[tb].bitcast(BF)
        for c in range(NCHK):
            tr(pT[0:64, c * 128:(c + 1) * 128], knT[:, h, c * 64:(c + 1) * 64], identb, [kn_b, B("identb")], [pbuf[tb]])
        kview = vw(pT[0:64, 0:NCHK * 128], 128)
        tt("dve", kbg, kview, c_bg[:, :, h].unsqueeze(2).broadcast_to([64, NCHK, 128]), ALU.mult, [pbuf[tb], BS("c_bg")], [B("kbg")])
        tt("dve", kdm[h], kview, c_kd[:, :, h].unsqueeze(2).broadcast_to([64, NCHK, 128]), ALU.mult, [pbuf[tb], BS("c_kd")], [B("kdm%d" % (h % 4))])
        tb = bank("t"); pT = pbank[tb].bitcast(BF)
        for c in range(NCHK):
            tr(pT[0:64, c * 128:(c + 1) * 128], vT[:, h, c * 64:(c + 1) * 64], identb, [v_b, B("identb")], [pbuf[tb]])
        tt("dve", vbm, vw(pT[0:64, 0:NCHK * 128], 128), beta[:, :, h].unsqueeze(2).broadcast_to([64, NCHK, 128]), ALU.mult,
           [pbuf[tb], BS("beta")], [B("vbm")])
        gu = bank("g")
        for c in range(NCHK):
            mm(pbank[gu][0:64, c * 128:(c + 1) * 128], Rm[:, c, :], vbm[:, c, :], True, True, [B("Rm" + sfx), B("vbm")], [pbuf[gu]])
        cp("act", Um[h], vw(pbank[gu][0:64, 0:NCHK * 128], 128), [pbuf[gu]], [B("Um%d" % (h % 4))])
        gw = bank("g")
        for c in range(NCHK):
            mm(pbank[gw][:, c * 64:(c + 1) * 64], kbg[:, c, :], Rm[:, c, :], True, True, [B("kbg"), B("Rm" + sfx)], [pbuf[gw]])
        cp("act", WT[h], vw(pbank[gw][:, 0:NCHK * 64]), [pbuf[gw]], [B("WT%d" % (h % 4))])

    ocnt = [0]

    def chain_step(c, h, own):
        sc_raw = SC("sc_raw"); beta = SC("beta"); gtok = SC("gtok"); etmp = SC("etmp"); gc = SC("gc"); glb = SC("glb"); eglb = SC("eglb"); c_bg = SC("c_bg"); c_kd = SC("c_kd"); c_eg = SC("c_eg"); negb = SC("negb"); tmp88 = SC("tmp88"); vcm = SC("vcm")
        Sbb = B("Sb_%d" % h); S32b = B("S32_%d" % h)
        g1 = bank("g")
        mm(pbank[g1][0:64, 0:128], WT[h][:, c, :], Sb[:, h, :], True, True, [B("WT%d" % (h % 4)), Sbb], [pbuf[g1]])
        tt("dve", vnew[h], Um[h][:, c, :], pbank[g1][0:64, 0:128], ALU.subtract, [B("Um%d" % (h % 4)), pbuf[g1]], [B("vnew%d" % (h % 4))])
        if own:
            i = ocnt[0] % 2; ocnt[0] += 1
            sl = slice(c * 64, (c + 1) * 64)
            go = bank("g")
            mm(pbank[go][0:64, 0:128], qnT[:, h, sl], Sb[:, h, :], True, True, [B("qnT_%d" % h), Sbb], [pbuf[go]])
            gi = bank("g")
            mm(pbank[gi][0:64, 0:128], AT[h][:, c, :], vnew[h], True, True, [B("AT%d" % (h % 4)), B("vnew%d" % (h % 4))], [pbuf[gi]])
            act(o1[i], pbank[go][0:64, 0:128], AF.Identity, [pbuf[go], BS("c_eg")], [B("o1_%d" % i)], scale=c_eg[:, c, h:h + 1])
            tt("dve", o1[i], o1[i], pbank[gi][0:64, 0:128], ALU.add, [B("o1_%d" % i), pbuf[gi]], [B("o1_%d" % i)])
            memset("pool", oss[:, i:i + 1], 0.0, [B("oss%d" % i)])
            act(ojunk, o1[i], AF.Square, [B("o1_%d" % i), B("oss%d" % i)], [B("ojunk"), B("oss%d" % i)], accum=oss[:, i:i + 1])
            rsqrt(oss[:, i:i + 1], oss[:, i:i + 1], 1.0 / 128, EPS, [B("oss%d" % i)], [B("oss%d" % i)])
            ts("dve", onb[i], o1[i], oss[:, i:i + 1], None, ALU.mult, None, [B("o1_%d" % i), B("oss%d" % i)], [B("onb%d" % i)])
            tb = bank("t"); pT = pbank[tb].bitcast(BF)
            tr(pT[:, 0:64], onb[i], identb[0:64, 0:64], [B("onb%d" % i), B("identb")], [pbuf[tb]])
            stt("dve", oT[:, h, sl], pT[:, 0:64], ggain[:, 0:1], zT[:, h, sl], ALU.mult, ALU.mult,
                [pbuf[tb], B("ggain"), B("zT_%d" % h)], [B("oT_%d" % h)])
        g2 = bank("g")
        mm(pbank[g2][:, 0:128], kdm[h][:, c, :], vnew[h], True, True, [B("kdm%d" % (h % 4)), B("vnew%d" % (h % 4))], [pbuf[g2]])
        stt("dve", S32[:, h, :], S32[:, h, :], eglb[:, c, h:h + 1], pbank[g2][:, 0:128], ALU.mult, ALU.add,
            [S32b, BS("eglb"), pbuf[g2]], [S32b])
        cp("act", Sb[:, h, :], S32[:, h, :], [S32b], [Sbb])

    def swa_kv(t):
        cp("pool", kbT[:, :, 0:128], kbT[:, :, TT:TT + 128], [B("kbT")], [B("kbT")])
        cp("pool", vtok[:, 0, :], vtok[:, NBLK, :], [B("vtok")], [B("vtok")])
        cp("pool", keyneg[:, 0:1], keyneg[:, NBLK:NBLK + 1], [B("keyneg")], [B("keyneg")])
        for j in range(2):
            wb, wbl = wstream(wkb_d, j * 256)
            for r in range(2):
                pb = proj_fm(wb, wbl, r * 128)
                rmsnorm64(pb, kg8[:, 0:1], B("kg8"), kbT[:, 2 * j + r, 128:128 + TT], [B("kbT")])
        for j in range(2):
            wb, wbl = wstream(wvb_d, j * 256)
            for b in range(NBLK):
                pb = bank("p")
                for kc in range(16):
                    mm(pbank[pb][:, 0:256], hT[:, kc, b * 128:(b + 1) * 128], wb[:, kc, :], kc == 0, kc == 15,
                       [wbl[kc], hbs[kc]], [pbuf[pb]])
                cp("act", vtok[:, 1 + b, j * 256:(j + 1) * 256], pbank[pb][:, 0:256], [pbuf[pb]], [B("vtok")])
        for b in range(NBLK):
            r0 = t * TT + b * 128
            dma("sp", tbuf[:, b:b + 1], validc_d[r0:r0 + 128, :], [], [B("tbuf")])
            ts("dve", keyneg[:, 1 + b:2 + b], tbuf[:, b:b + 1], BIG, -BIG, ALU.mult, ALU.add, [B("tbuf")], [B("keyneg")])

    def own_proj(q_only=False):
        for p in range(4):
            wb, wbl = wstream(wq_d, p * 256)
            for r in range(2):
                h = 2 * p + r
                pb = proj_fm(wb, wbl, r * 128)
                conv_chunk(pb, 16 + h, qnT[:, h, :], [B("qnT_%d" % h)], True, scale_q=128.0 ** -0.5)
        if q_only:
            return
        for p in range(4):
            wb, wbl = wstream(wz_d, p * 256)
            for r in range(2):
                h = 2 * p + r
                pb = proj_fm(wb, wbl, r * 128)
                act(zT[:, h, :], pbank[pb][:, 0:TT], AF.Silu, [pbuf[pb]], [B("zT_%d" % h)])

    def own_proj_swa():
        for p in range(4):
            wb, wbl = wstream(wqb_d, p * 256)
            for hh in range(4):
                hd = 4 * p + hh
                pb = bank("p")
                for kc in range(16):
                    mm(pbank[pb][0:64, 0:TT], wb[:, kc, hh * 64:(hh + 1) * 64], hT[:, kc, :], kc == 0, kc == 15,
                       [wbl[kc], hbs[kc]], [pbuf[pb]])
                act(sqk[0:64, :], pbank[pb][0:64, 0:TT], AF.Square, [pbuf[pb]], [B("sqk")])
                g = bank("g")
                mm(pbank[g][0:64, 0:TT], blk64[0:64, 0:64], sqk[0:64, :], True, True, [B("blk64"), B("sqk")], [pbuf[g]])
                rsqrt(rinv[0:64, :], pbank[g][0:64, 0:TT], 1.0, 64.0 * EPS, [pbuf[g]], [B("rinv")])
                stt("dve", q2T64[:, hd, :], pbank[pb][0:64, 0:TT], qg8[0:64, 0:1], rinv[0:64, :], ALU.mult, ALU.mult,
                    [pbuf[pb], B("rinv"), B("qg8")], [B("knT_%d" % hd if hd < 8 else "vT_%d" % (hd - 8))])
        for p in range(4):
            wb, wbl = wstream(wzb_d, p * 256)
            for r in range(2):
                pb = proj_fm(wb, wbl, r * 128)
                act(zbT[:, 2 * p + r, :], pbank[pb][:, 0:TT], AF.Silu, [pbuf[pb]], [B("zT_%d" % (2 * p + r))])

    def swa_attn():
        for p in range(8):
            g = p // 2
            bi = p % 2
            dma("sp", biasb[bi], biasT_d[:, p * 512:(p + 1) * 512], [], [B("biasb0")])
            for b in range(NBLK):
                qs = slice(b * 128, (b + 1) * 128)
                gl = bank("g")
                for r in range(2):
                    R = slice(64 * r, 64 * r + 64)
                    for kb in range(2):
                        ks = slice((b + kb) * 128, (b + kb + 1) * 128)
                        mm(pbank[gl][:, (r * 2 + kb) * 128:(r * 2 + kb + 1) * 128], kbT[0:64, g, ks], q2T64[:, 2 * p + r, qs],
                           True, True, [B("kbT"), B("knT_%d" % (2 * p + r) if 2 * p + r < 8 else "vT_%d" % (2 * p + r - 8))], [pbuf[gl]])
                stt("dve", tbuf, pbank[gl][:, :], 0.125, biasb[bi], ALU.mult, ALU.add, [pbuf[gl], B("biasb0")], [B("tbuf")])
                t4 = tbuf.rearrange("p (r k q) -> p r k q", r=2, k=2)
                p4 = pTb.rearrange("p (r k q) -> p r k q", r=2, k=2)
                for r in range(2):
                    for kb in range(2):
                        ix = slice((r * 2 + kb) * 128, (r * 2 + kb + 1) * 128)
                        act(pTb[:, ix], tbuf[:, ix], AF.Exp, [B("tbuf"), B("keyneg")], [B("pTb")],
                            bias=keyneg[:, b + kb:b + kb + 1])
                g2 = bank("g")
                for r in range(2):
                    for part in range(2):
                        c0 = r * 256 + part * 128
                        for kb in range(2):
                            lhs = vtok[:, b + kb, g * 128:(g + 1) * 128] if part == 0 else onesb
                            mm(pbank[g2][:, c0:c0 + 128], lhs, p4[:, r, kb, :], kb == 0, kb == 1,
                               [B("vtok"), B("onesb"), B("pTb")], [pbuf[g2]])
                for r in range(2):
                    R = slice(64 * r, 64 * r + 64)
                    hd = 2 * p + r
                    ts("dve", rden[R, :], pbank[g2][R, r * 256 + 128:r * 256 + 256], esink[R, hd:hd + 1], None, ALU.add, None,
                       [pbuf[g2], B("esink")], [B("rden")])
                    recip(rden[R, :], rden[R, :], [B("rden")], [B("rden")])
                    tt("dve", otmp[R, :], pbank[g2][R, r * 256:r * 256 + 128], rden[R, :], ALU.mult, [pbuf[g2], B("rden")], [B("otmp")])
                    tt("pool", obT[R, p, qs], otmp[R, :], zbT[R, p, qs], ALU.mult, [B("otmp"), B("zT_%d" % p)], [B("qnT_%d" % p)])

    wbb_i = [0]

    def final(t):
        oi = t - FIRST_OWN
        for cc in range(16):
            wb, wbl = wstream(wgab_d, cc * 256)
            i = wbb_i[0] % 2; wbb_i[0] += 1
            dma("pool", wbb[i], wbr_d[:, cc * 256:(cc + 1) * 256].rearrange("(k p) c -> p k c", p=128), [], wbb_bufs[i])
            pa = proj_fm(wb, wbl, 0)
            pbb = proj_fm(wb, wbl, 128)
            gy = bank("g")
            for h in range(8):
                mm(pbank[gy][:, 0:TT], wbb[i][:, h, 0:128], oT[:, h, :], h == 0, h == 7, wbb_bufs[i] + [B("oT_%d" % h)], [pbuf[gy]])
            gs = bank("g")
            for p in range(8):
                mm(pbank[gs][:, 0:TT], wbb[i][:, p, 128:256], obT[:, p, :], p == 0, p == 7, wbb_bufs[i] + [B("qnT_%d" % p)], [pbuf[gs]])
            act(sga, pbank[pa][:, 0:TT], AF.Sigmoid, [pbuf[pa]], [B("cacc")])
            act(sgb, pbank[pbb][:, 0:TT], AF.Sigmoid, [pbuf[pbb]], [B("cact2")])
            tt("dve", mt1, sga, pbank[gy][:, 0:TT], ALU.mult, [B("cacc"), pbuf[gy]], [B("rinv")])
            tt("dve", mt2, sgb, pbank[gs][:, 0:TT], ALU.mult, [B("cact2"), pbuf[gs]], [B("pre0")])
            tt("pool", mixedT[:, cc, :], mt1, mt2, ALU.add, [B("rinv"), B("pre0")], [B("mx_%d" % cc), B("xn%d" % (cc * TT // D))])
        ri = 0
        for cb in range(8):
            wb, wbl = wstream(wout_d, cb * 256)
            cs = slice(cb * 256, (cb + 1) * 256)
            for s in range(NBLK):
                pb = bank("p")
                for kc in range(16):
                    mm(pbank[pb][:, 0:256], mixedT[:, kc, s * 128:(s + 1) * 128], wb[:, kc, :], kc == 0, kc == 15,
                       [wbl[kc], B("mx_%d" % kc), B("xn%d" % (kc // 8))], [pbuf[pb]])
                i = ri % 2; ri += 1
                tt("dve", res[i], pbank[pb][:, 0:256], gate_row[:, cs], ALU.mult, [pbuf[pb], B("gate_row")], [B("tbuf")])
                tt("pool", xs[s][:, cs], xs[s][:, cs], res[i], ALU.add, [B("xs%d" % s), B("tbuf")], [B("xs%d" % s)])
        for s in range(NBLK):
            r0 = oi * TT + s * 128
            dma("sp", out_d[r0:r0 + 128, :], xs[s], [B("xs%d" % s)], [B("out_dram")])

    def S32M(h):
        return zT[:, h, :].bitcast(F32)

    def SbM(h):
        return oT[:, h, 0:128]

    def chain_step_M(c, h):
        sc_raw = SC("sc_raw"); beta = SC("beta"); gtok = SC("gtok"); etmp = SC("etmp"); gc = SC("gc"); glb = SC("glb"); eglb = SC("eglb"); c_bg = SC("c_bg"); c_kd = SC("c_kd"); c_eg = SC("c_eg"); negb = SC("negb"); tmp88 = SC("tmp88"); vcm = SC("vcm")
        zb_, ob_ = B("zT_%d" % h), B("oT_%d" % h)
        g1 = bank("g")
        mm(pbank[g1][0:64, 0:128], WT[h][:, c, :], SbM(h), True, True, [B("WT%d" % (h % 4)), ob_], [pbuf[g1]])
        ts("dve", vnewM[h], pbank[g1][0:64, 0:128], -1.0, None, ALU.mult, None, [pbuf[g1]], [B("vnewM%d" % (h % 4))])
        g2 = bank("g")
        mm(pbank[g2][:, 0:128], kdm[h][:, c, :], vnewM[h], True, True, [B("kdm%d" % (h % 4)), B("vnewM%d" % (h % 4))], [pbuf[g2]])
        stt("dve", S32M(h), S32M(h), eglb[:, c, h:h + 1], pbank[g2][:, 0:128], ALU.mult, ALU.add,
            [zb_, BS("eglb"), pbuf[g2]], [zb_])
        cp("act", SbM(h), S32M(h), [zb_], [ob_])

    def exchange_and_fold():
        k = 0
        for j in range(8):
            for h in range(8):
                st = tbuf[:, (k % 2) * 256:(k % 2) * 256 + 256]; k += 1
                ts("dve", st[:, 0:128], S32[:, h, :], selr[:, j:j + 1], None, ALU.mult, None,
                   [B("S32_%d" % h), B("selr"), B("tbuf")], [B("tbuf")])
                ts("dve", st[:, 128:256], S32M(h), selr[:, j:j + 1], None, ALU.mult, None,
                   [B("zT_%d" % h), B("selr"), B("tbuf")], [B("tbuf")])
                dma("sp", sum_src.ap()[j * 128:(j + 1) * 128, h * 256:(h + 1) * 256], st, [B("tbuf")], [B("sum_src")])
        P.op("pool", lambda e: e.collective_compute("AllReduce", ALU.add, replica_groups=[list(range(NCORES))],
                                                    ins=[sum_src.ap().opt()], outs=[sum_dst.ap().opt()]),
             reads=[B("sum_src")], writes=[B("sum_dst")], dma=True, cost=300.0, busy=5.0, inc=1)
        for h in range(8):
            memset("pool", S32[:, h, :], 0.0, [B("S32_%d" % h)])
            memset("pool", Sb[:, h, :], 0.0, [B("Sb_%d" % h)])
            for j in range(7):
                blk = tbuf[:, 0:256]
                dma("sp", blk, sum_dst.ap()[j * 128:(j + 1) * 128, h * 256:(h + 1) * 256], [B("sum_dst")], [B("tbuf")])
                gt = bank("g")
                P.op("pe", lambda e, o=pbank[gt][:, 0:128], i_=blk[:, 128:256]: e.transpose(o, i_, identf),
                     reads=[B("tbuf"), B("identf")], writes=[pbuf[gt]], cost=0.6, busy=0.3)
                cp("act", pTb[:, 0:128], pbank[gt][:, 0:128], [pbuf[gt]], [B("pTb")])
                g2 = bank("g")
                mm(pbank[g2][:, 0:128], pTb[:, 0:128], Sb[:, h, :], True, True, [B("pTb"), B("Sb_%d" % h)], [pbuf[g2]])
                tt("dve", otmp, pbank[g2][:, 0:128], blk[:, 0:128], ALU.add, [pbuf[g2], B("tbuf")], [B("otmp")])
                stt("dve", S32[:, h, :], otmp, selr[:, j + 1:j + 2], S32[:, h, :], ALU.mult, ALU.add,
                    [B("otmp"), B("selr"), B("S32_%d" % h)], [B("S32_%d" % h)])
                cp("act", Sb[:, h, :], otmp, [B("otmp")], [B("Sb_%d" % h)])
            cp("act", Sb[:, h, :], S32[:, h, :], [B("S32_%d" % h)], [B("Sb_%d" % h)])

    class Stop(Exception):
        pass

    def kv_path():
        for h in range(8):
            pb = proj_fm(wk, wkbl, h * 128)
            conv_chunk(pb, h, knT[:, h, :], [B("knT_%d" % h)], True)
        for h in range(8):
            pb = proj_fm(wv, wvbl, h * 128)
            conv_chunk(pb, 8 + h, vT[:, h, :], [B("vT_%d" % h)], False)

    def main_loop():
        for h in range(8):
            cp("dve", S32M(h), identf, [B("identf")], [B("zT_%d" % h)])
            cp("act", SbM(h), identf, [B("identf")], [B("oT_%d" % h)])
        for t in range(NT):
            cur_tp[0] = t % 2
            load_h(t)
            kv_path()
            if t < FIRST_OWN:
                continue
            scalars(t)
            for hg in range(2):
                for h in range(4 * hg, 4 * hg + 4):
                    head_state(h, False)
                for c in range(NCHK):
                    for h in range(4 * hg, 4 * hg + 4):
                        chain_step(c, h, False)
                        chain_step_M(c, h)
        if stop == 1: raise Stop()
        exchange_and_fold()
        if stop == 2: raise Stop()
        memset("pool", halo, 0.0, [B("halo%d" % c) for c in range(24)])
        for t in range(NT):
            own = t >= FIRST_OWN
            cur_tp[0] = t % 2
            load_h(t)
            kv_path()
            if t >= FIRST_OWN - 1:
                swa_kv(t)
            if not own:
                if t == FIRST_OWN - 1:
                    own_proj(q_only=True)
                continue
            scalars(t)
            own_proj()
            for hg in range(2):
                for h in range(4 * hg, 4 * hg + 4):
                    head_state(h, True)
                for c in range(NCHK):
                    for h in range(4 * hg, 4 * hg + 4):
                        chain_step(c, h, True)
            own_proj_swa()
            swa_attn()
            final(t)
        if debug:
            dma("sp", dbg["S"], S32.rearrange("p k t -> p (k t)"), [B("S32_%d" % h) for h in range(8)], [B("dbg_S")])

    def main_loop_replay():
      for t in range(NT):
        own = t >= FIRST_OWN
        if stop == 0: raise Stop()
        cur_tp[0] = t % 2
        load_h(t)
        if stop == 1: raise Stop()
        for h in range(8):
            pb = proj_fm(wk, wkbl, h * 128)
            conv_chunk(pb, h, knT[:, h, :], [B("knT_%d" % h)], True)
        for h in range(8):
            pb = proj_fm(wv, wvbl, h * 128)
            conv_chunk(pb, 8 + h, vT[:, h, :], [B("vT_%d" % h)], False)
        if stop == 2: raise Stop()
        scalars(t)
        if stop == 3: raise Stop()
        if t >= FIRST_OWN - 1:
            swa_kv(t)
        if own:
            own_proj()
        elif t == FIRST_OWN - 1:
            own_proj(q_only=True)
        for hg in range(2):
            for h in range(4 * hg, 4 * hg + 4):
                head_state(h, own)
            if stop == 4: raise Stop()
            for c in range(NCHK):
                for h in range(4 * hg, 4 * hg + 4):
                    chain_step(c, h, own)
        if stop == 5 + (1 if own else 0): raise Stop()
        if own:
            own_proj_swa()
            swa_attn()
            if stop == 7: raise Stop()
            final(t)
        if debug and t == NT - 1:
            dma("sp", dbg["hT"], hT.rearrange("p k t -> p (k t)"), hbs, [B("dbg_hT")])
            dma("sp", dbg["knT"], knT.rearrange("p k t -> p (k t)"), [B("knT_%d" % h) for h in range(8)], [B("dbg_knT")])
            dma("sp", dbg["vT"], vT.rearrange("p k t -> p (k t)"), [B("vT_%d" % h) for h in range(8)], [B("dbg_vT")])
            dma("sp", dbg["S"], S32.rearrange("p k t -> p (k t)"), [B("S32_%d" % h) for h in range(8)], [B("dbg_S")])
            dma("sp", dbg["oT"], oT.rearrange("p k t -> p (k t)"), [B("oT_%d" % h) for h in range(8)], [B("dbg_oT")])
            dma("sp", dbg["obT"], obT.rearrange("p k t -> p (k t)"), [B("qnT_%d" % h) for h in range(8)], [B("dbg_obT")])
            dma("sp", dbg["gc"], gc.rearrange("p c h -> p (c h)"), [B("gc")], [B("dbg_gc")])
            dma("sp", dbg["beta"], beta.rearrange("p c h -> p (c h)"), [B("beta")], [B("dbg_beta")])
            dma("sp", dbg["modP"], modP[:, 0:32], [B("modP")], [B("dbg_modP")])
            dma("sp", dbg["gate_row"], gate_row, [B("gate_row")], [B("dbg_gate_row")])
            dma("sp", dbg["cact"], cact, [B("cact")], [B("dbg_cact")])

    try:
        if MODE == "cc":
            main_loop()
        else:
            main_loop_replay()
    except Stop:
        pass
    P.emit(stack)
    stack.close()
    return nc


def _t5_bucket(dist):
    max_exact = 16
    d = np.maximum(dist, 1).astype(np.float32)
    large = max_exact + (np.log(d / max_exact) / np.log(128 / max_exact) * (32 - max_exact)).astype(np.int32)
    large = np.minimum(large, 31)
    return np.where(dist < max_exact, dist, large)


def prep_common(inp):
    f = np.float32
    l = 0
    w_in = np.asarray(inp["w_in"][l], f)
    offs = np.cumsum([0, 1024, 1024, 1024, 1024, 8, 8, 1024, 256, 256, 1024, 2048, 2048])
    seg = lambda i: w_in[:, offs[i]:offs[i + 1]]
    q_a, k_a, v_a, z_a, b_a, a_a, q_b, k_b, v_b, z_b, gl_a, gl_b = [seg(i) for i in range(12)]
    P = lambda v, n: np.ascontiguousarray(np.asarray(v, f).reshape(n, 128).T)
    rep = lambda v: np.ascontiguousarray(np.broadcast_to(np.asarray(v, f)[None, :], (128, len(v))))
    dup = lambda w: np.concatenate([np.concatenate([w[:, g * 64:(g + 1) * 64]] * 2, axis=1) for g in range(4)], axis=1)
    wgab = np.concatenate([np.concatenate([gl_a[:, c * 128:(c + 1) * 128], gl_b[:, c * 128:(c + 1) * 128]], axis=1)
                           for c in range(16)], axis=1)
    wg = np.asarray(inp["w_branch_gdn"][l], f); ws = np.asarray(inp["w_branch_swa"][l], f)
    wbr = np.concatenate([np.concatenate([wg[:, c * 128:(c + 1) * 128], ws[:, c * 128:(c + 1) * 128]], axis=1)
                          for c in range(16)], axis=1)
    conv = np.asarray(inp["conv_w"][l], f)
    conv_r = conv.reshape(4, 3, 8, 128)
    convw = np.stack([conv_r[:, 1], conv_r[:, 2], conv_r[:, 0]], axis=1).reshape(4, 24, 128).transpose(2, 1, 0)
    b_ada = np.asarray(inp["b_ada"][l], f)
    kk = np.arange(128)[:, None, None]; kb = np.arange(2)[None, :, None]; q = np.arange(128)[None, None, :]
    dist = 128 + q - kb * 128 - kk
    ok = (dist >= 0) & (dist < 128)
    rb = np.asarray(inp["rel_bias"], f)
    gathered = rb[_t5_bucket(np.maximum(dist, 0))]
    biasT = np.where(ok[..., None], gathered, f(-BIG)).transpose(0, 3, 1, 2)
    ii = np.arange(64)
    com = {
        "cT": P(inp["c"][0], 16), "w_ada": np.ascontiguousarray(inp["w_ada"][l], f),
        "badaP": P(b_ada, 48), "bgate_rep": rep(b_ada[4096:]), "bada_rep": rep(b_ada[:4096]), "gainP": P(inp["norm_gain"][l], 16),
        "wk": np.ascontiguousarray(k_a), "wv": np.ascontiguousarray(v_a),
        "wbg": np.ascontiguousarray(np.concatenate([b_a, a_a], axis=1)),
        "wq": np.ascontiguousarray(q_a), "wz": np.ascontiguousarray(z_a), "wqb": np.ascontiguousarray(q_b),
        "wkb_dup": np.ascontiguousarray(dup(k_b)), "wvb_dup": np.ascontiguousarray(dup(v_b)),
        "wzb": np.ascontiguousarray(z_b), "wgab": np.ascontiguousarray(wgab), "wbr": np.ascontiguousarray(wbr),
        "wout": np.ascontiguousarray(inp["w_out"][l], f),
        "convw": np.ascontiguousarray(convw), "alog_rep": rep(inp["a_log"][l]), "dtb_rep": rep(inp["dt_bias"][l]),
        "gdn_gainP": np.ascontiguousarray(np.asarray(inp["gdn_norm_gain"][l], f).reshape(128, 1)),
        "qgain2": np.ascontiguousarray(np.tile(np.asarray(inp["q_norm_gain"][l], f), 2).reshape(128, 1)),
        "kgain2": np.ascontiguousarray(np.tile(np.asarray(inp["k_norm_gain"][l], f), 2).reshape(128, 1)),
        "sinks_rep": rep(inp["sinks"][l]), "biasT": np.ascontiguousarray(biasT.reshape(128, -1), f),
        "identf": np.eye(128, dtype=f), "tri": np.triu(np.ones((128, 128), f)),
        "blk64": np.kron(np.eye(2, dtype=f), np.ones((64, 64), f)),
        "maskL": np.where(ii[None, :] < ii[:, None], f(0), f(BIG)).astype(f),
        "maskU": np.where(ii[None, :] >= ii[:, None], f(0), f(-BIG)).astype(f),
    }
    return com


def _validm(v, NT):
    return np.ascontiguousarray(v.reshape(NT, NCHK, 64).transpose(0, 2, 1).reshape(NT * 64, NCHK))


def prep_core(x2d, core, NT, NOWN, ncores):
    NTOK = NT * TT
    nown = NOWN * TT
    s0 = core * nown
    xp = np.zeros((NTOK, D), np.float32)
    v = np.zeros((NTOK,), np.float32)
    xp[NTOK - nown:] = x2d[s0:s0 + nown]
    v[NTOK - nown:] = 1.0
    nh = NTOK - nown
    if core > 0:
        xp[:nh] = x2d[s0 - nh:s0]
        v[:nh] = 1.0
    sel = np.zeros((128, 8), np.float32); sel[:, core] = 1.0
    return {"x": xp, "valid": np.ascontiguousarray(np.broadcast_to(v[None, :], (128, NTOK))),
            "validc": np.ascontiguousarray(v.reshape(NTOK, 1)), "sel_rep": sel, "validm": _validm(v, NT)}


def prep_core_replay(x2d, core, NT, NOWN, ncores):
    NTOK = NT * TT
    nreal = (core + 1) * NOWN * TT
    xp = np.zeros((NTOK, D), np.float32)
    xp[NTOK - nreal:] = x2d[:nreal]
    v = np.zeros((NTOK,), np.float32); v[NTOK - nreal:] = 1.0
    return {"x": xp, "valid": np.ascontiguousarray(np.broadcast_to(v[None, :], (128, NTOK))),
            "validc": np.ascontiguousarray(v.reshape(NTOK, 1)), "validm": _validm(v, NT)}


_CACHE = {}


def kernel(**inputs):
    x = np.asarray(inputs["x"], np.float32)
    T = x.shape[1]
    NOWN = T // NCORES // TT
    NT = NOWN + 1 if MODE == "cc" else T // TT
    if "nc" not in _CACHE:
        _CACHE["nc"] = build(NT, NOWN)
    nc = _CACHE["nc"]
    com = prep_common(inputs)
    in_maps = []
    for c in range(NCORES):
        m = dict(com)
        if MODE == "cc":
            m.update(prep_core(x[0], c, NT, NOWN, NCORES))
        else:
            m.update(prep_core_replay(x[0], c, NT, NOWN, NCORES))
            m["sel_rep"] = np.zeros((128, 8), np.float32)
        in_maps.append(m)
    res = run_bass_kernel_spmd(nc, in_maps, core_ids=list(range(NCORES)))
    out = np.concatenate([np.asarray(res.results[c]["out"], np.float32) for c in range(NCORES)], axis=0)
    return out.reshape(1, T, D)
```

```python
import numpy as np
from contextlib import ExitStack
import concourse.bass as bass
import concourse.mybir as mybir
from concourse.bass_utils import run_bass_kernel_spmd

F32 = mybir.dt.float32
BF = mybir.dt.bfloat16
ALU = mybir.AluOpType
AF = mybir.ActivationFunctionType

D = 2048
EPS = 1e-6
NCORES = 8
TT = 256
CH = 64
BIG = 30000.0
SCHED = True
PRIO_BL = True
TBL_AWARE = True
TBL_PEN = 1.3
UNIFIED_BANKS = False
NB_T, NB_P = 1, 2
MODE = "replay"


class Buf:
    __slots__ = ("name", "last_w", "readers", "dma_sem", "dma_cnt")

    def __init__(self, name):
        self.name = name
        self.last_w = None
        self.readers = []
        self.dma_sem = None
        self.dma_cnt = 0


class Op:
    __slots__ = ("eng", "fn", "deps", "is_dma", "sem", "val", "signal", "idx", "cost", "busy", "inc", "tbl")


class Prog:
    ENGS = ("pe", "act", "dve", "pool", "sp")

    def __init__(self, nc):
        self.nc = nc
        self.ops = []
        self.per_eng = {e: [] for e in self.ENGS}

    def op(self, eng, fn, reads=(), writes=(), dma=False, cost=0.3, busy=None, inc=16):
        o = Op()
        o.eng, o.fn, o.is_dma, o.signal, o.sem, o.val, o.idx = eng, fn, dma, False, None, 0, len(self.ops)
        o.cost = cost
        o.busy = cost if busy is None else busy
        o.tbl = 0
        deps = set()
        for b in reads:
            if b.last_w is not None:
                deps.add(b.last_w)
        for b in writes:
            if b.last_w is not None:
                deps.add(b.last_w)
            for r in b.readers:
                deps.add(r)
        deps.discard(o.idx)
        o.deps = sorted(deps)
        for b in reads:
            b.readers.append(o.idx)
        for b in writes:
            b.last_w = o.idx
            b.readers = []
        if dma:
            b = writes[0]
            o.sem = b
            b.dma_cnt += inc
            o.val = b.dma_cnt
            o.inc = inc
        self.ops.append(o)
        self.per_eng[eng].append(o)
        return o

    def schedule(self):
        ops = self.ops
        n = len(ops)
        indeg = [len(o.deps) for o in ops]
        succ = [[] for _ in range(n)]
        for o in ops:
            for d in o.deps:
                succ[d].append(o.idx)
        est = [0.0] * n
        fin = [0.0] * n
        bl = [0.0] * n
        if PRIO_BL:
            for i in range(n - 1, -1, -1):
                o = ops[i]
                m = 0.0
                for j in succ[i]:
                    if bl[j] > m:
                        m = bl[j]
                bl[i] = m + o.cost + 0.15
        cand = {e: [] for e in self.ENGS}
        for o in ops:
            if indeg[o.idx] == 0:
                cand[o.eng].append(o.idx)
        T = {e: 0.0 for e in self.ENGS}
        new = {e: [] for e in self.ENGS}
        cur_tbl = [0]; n_sw = [0]
        done = 0
        WIN = 48
        while done < n:
            best = None
            for e in self.ENGS:
                cl = cand[e]
                if not cl:
                    continue
                te = T[e]
                for k in range(min(len(cl), WIN)):
                    i = cl[k]
                    st = est[i] if est[i] > te else te
                    if TBL_AWARE and e == "act" and ops[i].tbl not in (0, cur_tbl[0]):
                        st += TBL_PEN
                    key = (st, -bl[i], i)
                    if best is None or key < best[0]:
                        best = (key, e, k)
            (st, _p, i), e, k = best
            cand[e].pop(k)
            o = ops[i]
            if e == "act" and o.tbl != 0:
                if o.tbl != cur_tbl[0]:
                    if not TBL_AWARE:
                        st += TBL_PEN
                    n_sw[0] += 1
                cur_tbl[0] = o.tbl
            T[e] = st + o.busy
            fin[i] = st + o.cost
            new[e].append(o)
            done += 1
            for j in succ[i]:
                if e == "pe" and ops[j].eng == "pe" and not o.is_dma and not ops[j].is_dma:
                    t2 = st + o.busy
                else:
                    t2 = fin[i] + (0.06 if ops[j].eng == e else 0.2)
                if t2 > est[j]:
                    est[j] = t2
                indeg[j] -= 1
                if indeg[j] == 0:
                    cl = cand[ops[j].eng]
                    lo, hi = 0, len(cl)
                    while lo < hi:
                        mid = (lo + hi) // 2
                        if cl[mid] < j:
                            lo = mid + 1
                        else:
                            hi = mid
                    cl.insert(lo, j)
        self.per_eng = new
        self.est_total = max(fin) if fin else 0.0
        self.n_tbl_switch = n_sw[0]

    def emit(self, stack):
        nc = self.nc
        ops = self.ops
        if SCHED:
            self.schedule()
        pos = {}
        for e in self.ENGS:
            for k, o in enumerate(self.per_eng[e]):
                pos[o.idx] = k
        for o in ops:
            latest = {}
            keep = []
            for d in o.deps:
                y = ops[d]
                if y.is_dma:
                    keep.append(d)
                    continue
                if y.eng == "pe" and o.eng == "pe" and not o.is_dma:
                    continue
                if y.eng not in latest or pos[d] > pos[latest[y.eng]]:
                    latest[y.eng] = d
            for d in latest.values():
                ops[d].signal = True
                keep.append(d)
            o.deps = keep
        esem = {}
        for e in ("pe", "act", "dve", "pool"):
            esem[e] = stack.enter_context(nc.semaphore("sem_" + e))
            n = 0
            for o in self.per_eng[e]:
                if o.is_dma:
                    continue
                if o.signal:
                    n += 1
                    o.sem, o.val = esem[e], n
        for o in ops:
            if o.is_dma:
                b = o.sem
                if b.dma_sem is None:
                    b.dma_sem = stack.enter_context(nc.semaphore("dsem_" + b.name))
                o.sem = b.dma_sem
        block = stack.enter_context(nc.Block())
        last_out = [o for o in ops if o.is_dma]

        def body(ename):
            def run(eng):
                waited = {}
                for o in self.per_eng[ename]:
                    for d in o.deps:
                        y = ops[d]
                        if (not y.is_dma) and y.eng == "pe" and ename == "pe" and not o.is_dma:
                            continue
                        key = id(y.sem)
                        if waited.get(key, 0) >= y.val:
                            continue
                        eng.wait_ge(y.sem, y.val)
                        waited[key] = y.val
                    ins = o.fn(eng)
                    if o.is_dma:
                        if o.inc == 16:
                            ins.then_inc(o.sem, 16)
                        else:
                            ins.then_inc(o.sem)
                    elif o.signal:
                        ins.then_inc(o.sem, 1)
                if ename == "sp":
                    fin = {}
                    for o in last_out:
                        fin[id(o.sem)] = (o.sem, max(o.val, fin.get(id(o.sem), (None, 0))[1]))
                    for sem, val in fin.values():
                        eng.wait_ge(sem, val)
            return run

        block.tensor(body("pe"))
        block.scalar(body("act"))
        block.vector(body("dve"))
        block.gpsimd(body("pool"))
        block.sync(body("sp"))


NCHK = TT // CH
NBLK = TT // 128


def build(NT, NOWN, debug=False, stop=99):
    nc = bass.Bass("TRN2", target_bir_lowering=False)
    NTOK = NT * TT
    stack = ExitStack()
    P = Prog(nc)
    FIRST_OWN = NT - NOWN

    def din(name, shape, dt=F32):
        return nc.dram_tensor(name, list(shape), dt, kind="ExternalInput").ap()

    def dout(name, shape, dt=F32):
        return nc.dram_tensor(name, list(shape), dt, kind="ExternalOutput").ap()

    x_d = din("x", [NTOK, D]); valid_d = din("valid", [128, NTOK]); validc_d = din("validc", [NTOK, 1])
    validm_d = din("validm", [NT * 64, NCHK])
    cT_d = din("cT", [128, 16]); wada_d = din("w_ada", [D, 3 * D]); badaP_d = din("badaP", [128, 48])
    bgate_d = din("bgate_rep", [128, D]); bada_rep_d = din("bada_rep", [128, 2 * D]); gainP_d = din("gainP", [128, 16])
    wk_d = din("wk", [D, 1024]); wv_d = din("wv", [D, 1024]); wbg_d = din("wbg", [D, 16])
    wq_d = din("wq", [D, 1024]); wz_d = din("wz", [D, 1024]); wqb_d = din("wqb", [D, 1024])
    wkb_d = din("wkb_dup", [D, 512]); wvb_d = din("wvb_dup", [D, 512]); wzb_d = din("wzb", [D, 1024])
    wgab_d = din("wgab", [D, 4096]); wbr_d = din("wbr", [1024, 4096]); wout_d = din("wout", [D, D])
    convw_d = din("convw", [128, 24, 4]); alog_d = din("alog_rep", [128, 8]); dtb_d = din("dtb_rep", [128, 8])
    ggain_d = din("gdn_gainP", [128, 1]); qg_d = din("qgain2", [128, 1]); kg_d = din("kgain2", [128, 1])
    sinks_d = din("sinks_rep", [128, 16]); biasT_d = din("biasT", [128, 16 * 2 * 128])
    identf_d = din("identf", [128, 128]); tri_d = din("tri", [128, 128]); blk64_d = din("blk64", [128, 128])
    maskL_d = din("maskL", [64, 64]); maskU_d = din("maskU", [64, 64])
    sel_d = din("sel_rep", [128, 8])
    sum_src = nc.dram_tensor("sum_src", [8 * 128, 8 * 256], F32)
    sum_dst = nc.dram_tensor("sum_dst", [8 * 128, 8 * 256], F32)
    out_d = dout("out", [NOWN * TT, D])
    dbg = {}
    if debug:
        dbg["hT"] = dout("dbg_hT", [128, 16 * TT], BF)
        dbg["knT"] = dout("dbg_knT", [128, 8 * TT], BF)
        dbg["vT"] = dout("dbg_vT", [128, 8 * TT], BF)
        dbg["S"] = dout("dbg_S", [128, 8 * 128])
        dbg["oT"] = dout("dbg_oT", [128, 8 * TT], BF)
        dbg["obT"] = dout("dbg_obT", [128, 8 * TT], BF)
        dbg["gc"] = dout("dbg_gc", [64, NCHK * 8])
        dbg["beta"] = dout("dbg_beta", [64, NCHK * 8])
        dbg["modP"] = dout("dbg_modP", [128, 32]); dbg["gate_row"] = dout("dbg_gate_row", [128, D], BF)
        dbg["cact"] = dout("dbg_cact", [128, 16], BF)

    def sb(name, shape, dt=F32):
        return nc.alloc_sbuf_tensor("s_" + name, list(shape), dt).ap()

    bufs = {}

    def B(name):
        if name not in bufs:
            bufs[name] = Buf(name)
        return bufs[name]

    identf = sb("identf", [128, 128]); identb = sb("identb", [128, 128], BF)
    onesf = sb("onesf", [128, 128]); onesb = sb("onesb", [128, 128], BF)
    tri = sb("tri", [128, 128]); blk64f = sb("blk64f", [128, 128]); blk64 = sb("blk64", [128, 128], BF)
    maskL = sb("maskL", [64, 64]); maskU = sb("maskU", [64, 64])
    convw = sb("convw", [128, 24, 4])
    alog = sb("alog", [128, 8]); dtb = sb("dtb", [128, 8]); negA = sb("negA", [128, 8])
    cT = sb("cT", [128, 16]); cact = sb("cact", [128, 16], BF); cactw = sb("cactw", [128, 16, 2], BF)
    badaP = sb("badaP", [128, 48]); modP = sb("modP", [128, 48]); gainP = sb("gainP", [128, 16])
    gammaP = sb("gammaP", [128, 16]); gate_row = sb("gate_row", [128, D], BF)
    ggain = sb("ggain", [128, 1]); qg8 = sb("qg8", [128, 1]); kg8 = sb("kg8", [128, 1])
    esink = sb("esink", [128, 16])
    wk = sb("wk", [128, 16, 1024], BF); wv = sb("wv", [128, 16, 1024], BF); wbg = sb("wbg", [128, 16, 16], BF)
    wst = [sb("wst%d" % i, [128, 16, 256], BF) for i in range(2)]
    wbb0_ = sb("wbb0", [128, 8, 256], BF)
    xs = [sb("xs%d" % i, [128, D]) for i in range(2)]
    ss = sb("ss", [128, 2]); rstd = sb("rstd", [128, 2])
    xn = sb("xn", [128, NBLK, D], BF)
    mixedT = xn.rearrange("p a b -> p (a b)").rearrange("p (k t) -> p k t", t=TT)
    hT = sb("hT", [128, 16, TT], BF)
    validt = sb("validt", [128, TT])
    halo = sb("halo", [128, 24, 3])
    pre = [sb("pre%d" % i, [128, TT + 3]) for i in range(2)]
    caccs = [sb("cacc%d" % i, [128, TT]) for i in range(2)]; cact2s = [sb("cact2_%d" % i, [128, TT]) for i in range(2)]
    sqks = [sb("sqk%d" % i, [128, TT], BF) for i in range(2)]; rinvs = [sb("rinv%d" % i, [128, TT]) for i in range(2)]
    cacc, cact2, sqk, rinv = caccs[0], cact2s[0], sqks[0], rinvs[0]
    kvT = sb("kvT", [128, 16, TT], BF); knT = kvT[:, 0:8, :]; vT = kvT[:, 8:16, :]
    wbb = [wbb0_, kvT[:, 0:8, :]]
    wbb_bufs = [[B("wbb0")], [B("knT_%d" % h) for h in range(8)]]
    SCB = {}
    for _nm, _shp in (("sc_raw", [64, NCHK, 16]), ("beta", [64, NCHK, 8]), ("gtok", [64, NCHK, 8]), ("etmp", [64, NCHK, 8]),
                      ("gc", [64, NCHK, 8]), ("glb", [128, NCHK, 8]), ("eglb", [128, NCHK, 8]), ("c_bg", [64, NCHK, 8]),
                      ("c_kd", [64, NCHK, 8]), ("c_eg", [64, NCHK, 8]), ("negb", [64, NCHK, 8]), ("tmp88", [64, NCHK, 8]),
                      ("vcm", [64, NCHK])):
        SCB[_nm] = [sb("%s_p%d" % (_nm, i), _shp) for i in range(2)]
    cur_tp = [0]

    def SC(nm):
        return SCB[nm][cur_tp[0]]

    def BS(nm):
        return B("%s_p%d" % (nm, cur_tp[0]))
    NHB = 3
    dghs = [sb("dgh0", [64, NCHK, 64])] * NHB
    args = [sb("arg%d" % i, [64, NCHK, 64]) for i in range(NHB)]; Dms = [sb("Dm%d" % i, [64, NCHK, 64]) for i in range(NHB)]
    Nms = [sb("Nm%d" % i, [64, NCHK, 64], BF) for i in range(NHB)]; Pms = [sb("Pm%d" % i, [64, NCHK, 64], BF) for i in range(NHB)]
    Qms = [sb("Qm%d" % i, [64, NCHK, 64], BF) for i in range(NHB)]; Rms = [sb("Rm%d" % i, [64, NCHK, 64], BF) for i in range(NHB)]
    Rfs = [None] * NHB
    kbgs = [sb("kbg0", [64, NCHK, 128], BF)] * NHB; vbms = [sb("vbm0", [64, NCHK, 128], BF)] * NHB
    kdm = [sb("kdm%d" % h, [64, NCHK, 128], BF) for h in range(4)] * 2
    Um = [sb("Um%d" % h, [64, NCHK, 128], BF) for h in range(4)] * 2
    WT = [sb("WT%d" % h, [128, NCHK, 64], BF) for h in range(4)] * 2
    vnew = [sb("vnew%d" % h, [64, 128], BF) for h in range(4)] * 2
    S32 = sb("S32", [128, 8, 128]); Sb = sb("Sb", [128, 8, 128], BF)
    vnewM = [sb("vnewM%d" % h, [64, 128], BF) for h in range(4)] * 2
    selr = sb("selr", [128, 8])
    qnT = sb("qnT", [128, 8, TT], BF); zT = sb("zT", [128, 8, TT], BF)
    AT = [sb("AT%d" % h, [64, NCHK, 64], BF) for h in range(4)] * 2
    oT = sb("oT", [128, 8, TT], BF); obT = qnT
    q2T64 = kvT[0:64, :, :]; zbT = zT
    kbT = sb("kbT", [128, 4, 128 + TT], BF); vtok = sb("vtok", [128, 1 + NBLK, 512], BF)
    keyneg = sb("keyneg", [128, 1 + NBLK])
    biasb = [sb("biasb0", [128, 512])] * 2
    tbuf = sb("tbuf", [128, 512]); pTb = sb("pTb", [128, 512], BF)
    rden = sb("rden", [128, 128]); otmp = sb("otmp", [128, 128])
    o1 = [sb("o1_%d" % i, [64, 128]) for i in range(2)]
    ojunk = sb("ojunk", [64, 128], BF); oss = sb("oss", [64, 2]); onb = [sb("onb%d" % i, [64, 128], BF) for i in range(2)]
    sga = cacc; sgb = cact2; mt1 = rinv; mt2 = pre[0][:, 0:TT]
    res = [tbuf[:, 0:256], tbuf[:, 256:512]]

    pbank = [nc.alloc_psum_tensor("pb%d" % i, [128, 512], F32).ap() for i in range(8)]
    pbuf = [B("pb%d" % i) for i in range(8)]
    rr = {"g": 0, "p": 0, "t": 0}

    def bank(kind):
        if UNIFIED_BANKS:
            i = rr["g"]; rr["g"] = (i + 1) % 8; return i
        if kind == "t" and NB_T > 0:
            i = rr["t"]; rr["t"] = (i + 1) % NB_T; return i
        if kind == "p" and NB_P > 0:
            i = NB_T + rr["p"]; rr["p"] = (rr["p"] + 1) % NB_P; return i
        i = NB_T + NB_P + rr["g"]; rr["g"] = (rr["g"] + 1) % (8 - NB_T - NB_P); return i

    def nfree(ap):
        n = 1
        for d_ in ap.shape[1:]:
            n *= d_
        return n

    def dma(eng, out, in_, reads, writes):
        nb = nfree(out) * out.shape[0] * 4
        P.op(eng, lambda e: e.dma_start(out=out, in_=in_), reads=reads, writes=writes, dma=True,
             cost=2.0 + nb / 180e3, busy=0.15 if eng == "sp" else 1.0)

    def act(out, in_, func, reads, writes, bias=None, scale=None, accum=None):
        kw = {}
        if bias is not None: kw["bias"] = bias
        if scale is not None: kw["scale"] = scale
        if accum is not None: kw["accum_out"] = accum
        o_ = P.op("act", lambda e: e.activation(out, in_, func, **kw), reads=reads, writes=writes, cost=0.22 + nfree(out) / 1100.0)
        o_.tbl = 1 if func == AF.Silu else (2 if func in (AF.Exp, AF.Ln) else (3 if func == AF.Sigmoid else 0))

    def vcost(eng, out):
        return (0.08 + nfree(out) / 900.0) if eng == "dve" else (0.15 + nfree(out) / 450.0)

    def ts(eng, out, in0, s1, s2, op0, op1, reads, writes):
        if op1 is None:
            P.op(eng, lambda e: e.tensor_scalar(out, in0, s1, 0.0, op0, ALU.add), reads=reads, writes=writes, cost=vcost(eng, out))
        else:
            P.op(eng, lambda e: e.tensor_scalar(out, in0, s1, s2, op0, op1), reads=reads, writes=writes, cost=vcost(eng, out))

    def recip(out, in_, reads, writes):
        P.op("dve", lambda e: e.reciprocal(out, in_), reads=reads, writes=writes, cost=vcost("dve", out))

    def rsqrt(out, in0, s_mul, s_add, reads, writes):
        ts("dve", out, in0, s_mul, s_add, ALU.mult, ALU.add, reads, writes)
        act(out, out, AF.Ln, writes, writes)
        act(out, out, AF.Exp, writes, writes, scale=-0.5)

    def tt(eng, out, in0, in1, op, reads, writes):
        P.op(eng, lambda e: e.tensor_tensor(out, in0, in1, op), reads=reads, writes=writes, cost=vcost(eng, out))

    def stt(eng, out, in0, scalar, in1, op0, op1, reads, writes):
        P.op(eng, lambda e: e.scalar_tensor_tensor(out, in0, scalar, in1, op0, op1), reads=reads, writes=writes, cost=vcost(eng, out))

    def cp(eng, out, in_, reads, writes):
        if eng == "act":
            P.op(eng, lambda e: e.activation(out, in_, AF.Copy), reads=reads, writes=writes, cost=0.22 + nfree(out) / 1100.0)
        else:
            P.op(eng, lambda e: e.tensor_copy(out, in_), reads=reads, writes=writes, cost=vcost(eng, out))

    def mm(out, lhsT, rhs, start, stop, reads, writes):
        npass = 4 if "float32" in str(rhs.dtype) else 1
        P.op("pe", lambda e: e.matmul(out, lhsT, rhs, start=start, stop=stop), reads=reads, writes=writes,
             cost=0.25 + nfree(rhs) * npass / 2400.0, busy=0.035 + nfree(rhs) * npass / 2400.0)

    def tr(out, in_, ident, reads, writes):
        P.op("pe", lambda e: e.transpose(out, in_, ident), reads=reads, writes=writes, cost=0.3, busy=0.07)

    def memset(eng, ap, val, writes):
        P.op(eng, lambda e: e.memset(ap, val), reads=[], writes=writes, cost=vcost(eng, ap))

    wst_i = [0]

    def wstream(src_d, c0, rows=D):
        i = wst_i[0] % 2; wst_i[0] += 1
        src = src_d[:, c0:c0 + 256].rearrange("(k p) c -> p k c", p=128)
        bl = []
        for k0 in range(0, 16, 8):
            b = B("wst%d_%d" % (i, k0)); bl.append(b)
            dma("pool", wst[i][:, k0:k0 + 8, :], src[:, k0:k0 + 8, :], [], [b])
        return wst[i], [bl[kc // 8] for kc in range(16)]

    for (dst, src, nm) in [(identf, identf_d, "identf"), (tri, tri_d, "tri"), (blk64f, blk64_d, "blk64f"),
                           (maskL, maskL_d, "maskL"), (maskU, maskU_d, "maskU"), (convw, convw_d, "convw"),
                           (alog, alog_d, "alog"), (dtb, dtb_d, "dtb"), (cT, cT_d, "cT"), (badaP, badaP_d, "badaP"),
                           (gainP, gainP_d, "gainP"), (ggain, ggain_d, "ggain"),
                           (qg8, qg_d, "qg8"), (kg8, kg_d, "kg8"), (esink, sinks_d, "esink")] + ([(selr, sel_d, "selr")] if MODE == "cc" else []):
        dma("sp", dst, src, [], [B(nm)])
    dma("pool", gate_row, bgate_d, [], [B("gate_row")])
    cp("dve", identb, identf, [B("identf")], [B("identb")])
    cp("dve", blk64, blk64f, [B("blk64f")], [B("blk64")])
    memset("pool", onesf, 1.0, [B("onesf")]); memset("pool", onesb, 1.0, [B("onesb")])
    memset("pool", halo, 0.0, [B("halo%d" % c) for c in range(24)])
    memset("pool", S32, 0.0, [B("S32_%d" % h) for h in range(8)])
    memset("pool", Sb, 0.0, [B("Sb_%d" % h) for h in range(8)])
    memset("pool", kbT, 0.0, [B("kbT")]); memset("pool", vtok, 0.0, [B("vtok")]); memset("pool", keyneg, -BIG, [B("keyneg")])
    for (dst, src, nm) in [(wk, wk_d, "wk"), (wv, wv_d, "wv")]:
        s_ = src.rearrange("(k p) c -> p k c", p=128)
        for k0 in range(0, 16, 4):
            dma("pool", dst[:, k0:k0 + 4, :], s_[:, k0:k0 + 4, :], [], [B(nm + "_%d" % k0)])
    dma("pool", wbg, wbg_d.rearrange("(k p) c -> p k c", p=128), [], [B("wbg")])
    act(negA, alog, AF.Exp, [B("alog")], [B("negA")])
    ts("dve", negA, negA, -1.0, None, ALU.mult, None, [B("negA")], [B("negA")])
    ts("dve", qg8, qg8, 8.0, None, ALU.mult, None, [B("qg8")], [B("qg8")])
    ts("dve", kg8, kg8, 8.0, None, ALU.mult, None, [B("kg8")], [B("kg8")])
    act(esink, esink, AF.Exp, [B("esink")], [B("esink")])

    act(cact, cT, AF.Silu, [B("cT")], [B("cact")])
    cactrep = xn[:, 0, :].rearrange("p (k m) -> p k m", m=128)
    cp("dve", cactrep, cact.unsqueeze(2).broadcast_to([128, 16, 128]), [B("cact")], [B("xn0")])
    cp("dve", cactw, cact.unsqueeze(2).broadcast_to([128, 16, 2]), [B("cact")], [B("cactw")])
    adab = tbuf[:, 0:256]; adat = otmp
    for blk in range(24):
        wb, wbl = wstream(wada_d, blk * 256)
        pg = bank("g")
        for kc in range(16):
            mm(pbank[pg][:, 0:256], cactrep[:, kc, :], wb[:, kc, :], kc == 0, kc == 15,
               [wbl[kc], B("xn0")], [pbuf[pg]])
        if blk >= 16:
            c0 = (blk - 16) * 256
            tt("dve", gate_row[:, c0:c0 + 256], pbank[pg][:, 0:256], gate_row[:, c0:c0 + 256], ALU.add,
               [pbuf[pg], B("gate_row")], [B("gate_row")])
        else:
            dma("sp", adab, bada_rep_d[:, blk * 256:(blk + 1) * 256], [], [B("tbuf")])
            tt("dve", adab, pbank[pg][:, 0:256], adab, ALU.add, [pbuf[pg], B("tbuf")], [B("tbuf")])
            for j in range(2):
                col = blk * 2 + j
                tt("dve", adat, adab[:, j * 128:(j + 1) * 128], identf, ALU.mult, [B("tbuf"), B("identf")], [B("otmp")])
                P.op("dve", lambda e, o=modP[:, col:col + 1]: e.tensor_reduce(o, adat, mybir.AxisListType.X, ALU.add),
                     reads=[B("otmp")], writes=[B("modP")])
    stt("dve", gammaP, modP[:, 16:32], 1.0, gainP, ALU.add, ALU.mult, [B("modP"), B("gainP")], [B("gammaP")])

    hbs = [B("hT_%d" % kc) for kc in range(16)]

    def load_h(t):
        for s in range(NBLK):
            r0 = t * TT + s * 128
            dma("sp", xs[s], x_d[r0:r0 + 128, :], [], [B("xs%d" % s)])
            memset("pool", ss[:, s:s + 1], 0.0, [B("ss%d" % s)])
            act(xn[:, s, :], xs[s], AF.Square, [B("xs%d" % s), B("ss%d" % s)], [B("xn%d" % s), B("ss%d" % s)],
                accum=ss[:, s:s + 1])
            rsqrt(rstd[:, s:s + 1], ss[:, s:s + 1], 1.0 / D, EPS, [B("ss%d" % s)], [B("rstd%d" % s)])
            ts("dve", xn[:, s, :], xs[s], rstd[:, s:s + 1], None, ALU.mult, None,
               [B("xs%d" % s), B("rstd%d" % s)], [B("xn%d" % s)])
        for kc in range(16):
            tb = bank("t")
            pT = pbank[tb].bitcast(BF)
            for s in range(NBLK):
                tr(pT[:, s * 128:(s + 1) * 128], xn[:, s, kc * 128:(kc + 1) * 128], identb,
                   [B("xn%d" % s), B("identb")], [pbuf[tb]])
            if kc % 2 == 0:
                act(hT[:, kc, :], pT[:, 0:TT], AF.Identity, [pbuf[tb], B("gammaP"), B("modP")], [hbs[kc]],
                    bias=modP[:, kc:kc + 1], scale=gammaP[:, kc:kc + 1])
            else:
                ts("dve", hT[:, kc, :], pT[:, 0:TT], gammaP[:, kc:kc + 1], modP[:, kc:kc + 1], ALU.mult, ALU.add,
                   [pbuf[tb], B("gammaP"), B("modP")], [hbs[kc]])
        dma("sp", validt, valid_d[:, t * TT:(t + 1) * TT], [], [B("validt")])

    cc_count = [0]

    def conv_chunk(pb, cc, dst, dst_bufs, l2, scale_q=None):
        i = cc_count[0] % 2; cc_count[0] += 1
        pr = pre[i]; bp = B("pre%d" % i)
        cacc, cact2, sqk, rinv = caccs[i], cact2s[i], sqks[i], rinvs[i]
        ba = B("cacc" if i == 0 else "cacc1"); b2 = B("cact2" if i == 0 else "cact2_1")
        bsq = B("sqk" if i == 0 else "sqk1"); bri = B("rinv" if i == 0 else "rinv1")
        cp("pool", pr[:, 0:3], halo[:, cc, :], [B("halo%d" % cc)], [bp])
        cp("act", pr[:, 3:TT + 3], pbank[pb][:, 0:TT], [pbuf[pb], bp], [bp])
        tt("pool", halo[:, cc, :], pr[:, TT:TT + 3], validt[:, TT - 3:TT], ALU.mult, [bp, B("validt")], [B("halo%d" % cc)])
        ts("dve", cacc, pr[:, 0:TT], convw[:, cc, 0:1], None, ALU.mult, None, [bp, B("convw")], [ba])
        stt("dve", cacc, pr[:, 1:TT + 1], convw[:, cc, 1:2], cacc, ALU.mult, ALU.add, [bp, ba, B("convw")], [ba])
        stt("dve", cacc, pr[:, 2:TT + 2], convw[:, cc, 2:3], cacc, ALU.mult, ALU.add, [bp, ba, B("convw")], [ba])
        stt("dve", cacc, pr[:, 3:TT + 3], convw[:, cc, 3:4], cacc, ALU.mult, ALU.add, [bp, ba, B("convw")], [ba])
        if not l2:
            act(dst, cacc, AF.Silu, [ba], dst_bufs)
            return
        act(cact2, cacc, AF.Silu, [ba], [b2])
        act(sqk, cact2, AF.Square, [b2], [bsq])
        g = bank("g")
        mm(pbank[g][:, 0:TT], onesb, sqk, True, True, [B("onesb"), bsq], [pbuf[g]])
        rsqrt(rinv, pbank[g][:, 0:TT], 1.0, EPS, [pbuf[g]], [bri])
        if scale_q is None:
            tt("pool", dst, cact2, rinv, ALU.mult, [b2, bri], dst_bufs)
        else:
            stt("dve", dst, cact2, scale_q, rinv, ALU.mult, ALU.mult, [b2, bri], dst_bufs)

    def proj_fm(w, wbl, c0):
        pb = bank("p")
        for kc in range(16):
            mm(pbank[pb][:, 0:TT], w[:, kc, c0:c0 + 128], hT[:, kc, :], kc == 0, kc == 15, [wbl[kc], hbs[kc]], [pbuf[pb]])
        return pb

    def rmsnorm64(pb, gcol, gbuf, dst, dst_bufs):
        act(sqk, pbank[pb][:, 0:TT], AF.Square, [pbuf[pb]], [B("sqk")])
        g = bank("g")
        mm(pbank[g][:, 0:TT], blk64, sqk, True, True, [B("blk64"), B("sqk")], [pbuf[g]])
        rsqrt(rinv, pbank[g][:, 0:TT], 1.0, 64.0 * EPS, [pbuf[g]], [B("rinv")])
        stt("dve", dst, pbank[pb][:, 0:TT], gcol, rinv, ALU.mult, ALU.mult, [pbuf[pb], B("rinv"), gbuf], dst_bufs)

    wkbl = [B("wk_%d" % (kc // 4 * 4)) for kc in range(16)]
    wvbl = [B("wv_%d" % (kc // 4 * 4)) for kc in range(16)]

    def scalars(t):
        sc_raw = SC("sc_raw"); beta = SC("beta"); gtok = SC("gtok"); etmp = SC("etmp"); gc = SC("gc"); glb = SC("glb"); eglb = SC("eglb"); c_bg = SC("c_bg"); c_kd = SC("c_kd"); c_eg = SC("c_eg"); negb = SC("negb"); tmp88 = SC("tmp88"); vcm = SC("vcm")
        g0 = bank("g")
        for c in range(NCHK):
            for kc in range(16):
                mm(pbank[g0][0:64, c * 16:(c + 1) * 16], hT[:, kc, c * 64:(c + 1) * 64], wbg[:, kc, :],
                   kc == 0, kc == 15, [hbs[kc], B("wbg")], [pbuf[g0]])
        scb = BS("sc_raw")
        cp("dve", sc_raw, pbank[g0][0:64, 0:NCHK * 16].rearrange("p (c k) -> p c k", k=16), [pbuf[g0]], [scb])
        act(etmp, sc_raw[:, :, 0:8], AF.Exp, [scb], [BS("etmp")], scale=-1.0)
        ts("dve", etmp, etmp, 1.0, None, ALU.add, None, [BS("etmp")], [BS("etmp")])
        recip(etmp, etmp, [BS("etmp")], [BS("etmp")])
        dma("sp", vcm, validm_d[t * 64:(t + 1) * 64, :], [], [BS("vcm")])
        tt("dve", beta, etmp, vcm.unsqueeze(2).broadcast_to([64, NCHK, 8]), ALU.mult, [BS("etmp"), BS("vcm")], [BS("beta")])
        tt("dve", gtok, sc_raw[:, :, 8:16], dtb[0:64, :].unsqueeze(1).broadcast_to([64, NCHK, 8]), ALU.add,
           [scb, B("dtb")], [BS("gtok")])
        act(gtok, gtok, AF.Exp, [BS("gtok")], [BS("gtok")])
        ts("dve", gtok, gtok, 1.0, None, ALU.add, None, [BS("gtok")], [BS("gtok")])
        act(gtok, gtok, AF.Ln, [BS("gtok")], [BS("gtok")])
        tt("dve", gtok, gtok, negA[0:64, :].unsqueeze(1).broadcast_to([64, NCHK, 8]), ALU.mult,
           [BS("gtok"), B("negA")], [BS("gtok")])
        g1 = bank("g")
        gflat = gtok.rearrange("p c h -> p (c h)")
        NW = NCHK * 8
        mm(pbank[g1][0:64, 0:NW], tri[0:64, 0:64], gflat, True, True, [B("tri"), BS("gtok")], [pbuf[g1]])
        mm(pbank[g1][:, 64:64 + NW], onesf[0:64, :], gflat, True, True, [B("onesf"), BS("gtok")], [pbuf[g1]])
        cp("dve", gc.rearrange("p c h -> p (c h)"), pbank[g1][0:64, 0:NW], [pbuf[g1]], [BS("gc")])
        cp("dve", glb.rearrange("p c h -> p (c h)"), pbank[g1][:, 64:64 + NW], [pbuf[g1]], [BS("glb")])
        act(eglb, glb, AF.Exp, [BS("glb")], [BS("eglb")])
        act(c_eg, gc, AF.Exp, [BS("gc")], [BS("c_eg")])
        tt("dve", c_bg, c_eg, beta, ALU.mult, [BS("c_eg"), BS("beta")], [BS("c_bg")])
        tt("dve", tmp88, glb[0:64], gc, ALU.subtract, [BS("glb"), BS("gc")], [BS("tmp88")])
        act(c_kd, tmp88, AF.Exp, [BS("tmp88")], [BS("c_kd")])
        ts("dve", negb, beta, -1.0, None, ALU.mult, None, [BS("beta")], [BS("negb")])

    def vw(ap, n=64):
        return ap.rearrange("p (c j) -> p c j", j=n)

    def head_state(h, own):
        sc_raw = SC("sc_raw"); beta = SC("beta"); gtok = SC("gtok"); etmp = SC("etmp"); gc = SC("gc"); glb = SC("glb"); eglb = SC("eglb"); c_bg = SC("c_bg"); c_kd = SC("c_kd"); c_eg = SC("c_eg"); negb = SC("negb"); tmp88 = SC("tmp88"); vcm = SC("vcm")
        hi = h % NHB
        dgh, arg, Dm, Nm, Pm, Qm, Rm, Rf, kbg, vbm = (dghs[hi], args[hi], Dms[hi], Nms[hi], Pms[hi], Qms[hi], Rms[hi],
                                                    Rfs[hi], kbgs[hi], vbms[hi])
        sfx = "_%d" % hi
        kn_b = B("knT_%d" % h); v_b = B("vT_%d" % h)
        tt("dve", dgh, identf[0:64, 0:64].unsqueeze(1).broadcast_to([64, NCHK, 64]),
           gc[:, :, h].unsqueeze(2).broadcast_to([64, NCHK, 64]), ALU.mult, [B("identf"), BS("gc")], [B("dgh")])
        g0 = bank("g")
        mm(pbank[g0][0:64, 0:NCHK * 64], onesf[0:64, 0:64], dgh.rearrange("p c j -> p (c j)"), True, True,
           [B("onesf"), B("dgh")], [pbuf[g0]])
        gcb = gc[:, :, h].unsqueeze(2).broadcast_to([64, NCHK, 64])
        tt("dve", arg, vw(pbank[g0][0:64, 0:NCHK * 64]), gcb, ALU.subtract, [pbuf[g0], BS("gc")], [B("arg" + sfx)])
        if own:
            stt("dve", Dm, arg, 0.0, maskU.unsqueeze(1).broadcast_to([64, NCHK, 64]), ALU.min, ALU.add,
                [B("arg" + sfx), B("maskU")], [B("Dm" + sfx)])
            act(Dm, Dm, AF.Exp, [B("Dm" + sfx)], [B("Dm" + sfx)])
            g2 = bank("g")
            for c in range(NCHK):
                sl = slice(c * 64, (c + 1) * 64)
                mm(pbank[g2][0:64, sl], knT[:, h, sl], qnT[:, h, sl], True, True, [kn_b, B("qnT_%d" % h)], [pbuf[g2]])
            tt("dve", AT[h], vw(pbank[g2][0:64, 0:NCHK * 64]), Dm, ALU.mult, [pbuf[g2], B("Dm" + sfx)], [B("AT%d" % (h % 4))])
        stt("dve", arg, arg, 0.0, maskL.unsqueeze(1).broadcast_to([64, NCHK, 64]), ALU.max, ALU.add,
            [B("arg" + sfx), B("maskL")], [B("arg" + sfx)])
        act(Dm, arg, AF.Exp, [B("arg" + sfx)], [B("Dm" + sfx)], scale=-1.0)
        g1 = bank("g")
        for c in range(NCHK):
            sl = slice(c * 64, (c + 1) * 64)
            mm(pbank[g1][0:64, sl], knT[:, h, sl], knT[:, h, sl], True, True, [kn_b], [pbuf[g1]])
        tt("dve", Dm, Dm, negb[:, :, h].unsqueeze(2).broadcast_to([64, NCHK, 64]), ALU.mult, [B("Dm" + sfx), BS("negb")], [B("Dm" + sfx)])
        tt("dve", Nm, vw(pbank[g1][0:64, 0:NCHK * 64]), Dm, ALU.mult, [pbuf[g1], B("Dm" + sfx)], [B("Nm" + sfx)])
        tb = bank("t"); pT = pbank[tb].bitcast(BF)
        for c in range(NCHK):
            tr(pT[0:64, c * 64:(c + 1) * 64], Nm[:, c, :], identb[0:64, 0:64], [B("Nm" + sfx), B("identb")], [pbuf[tb]])
        cp("act", Pm, vw(pT[0:64, 0:NCHK * 64]), [pbuf[tb]], [B("Pm" + sfx)])
        tt("dve", Rm, Pm, identf[0:64, 0:64].unsqueeze(1).broadcast_to([64, NCHK, 64]), ALU.add,
           [B("Pm" + sfx), B("identf")], [B("Rm" + sfx)])
        Qc, Qn = Nm, "Nm" + sfx
        for lvl in range(5):
            gp = bank("g"); gq = bank("g")
            if lvl < 4:
                for c in range(NCHK):
                    sl = slice(c * 64, (c + 1) * 64)
                    mm(pbank[gp][0:64, sl], Qc[:, c, :], Pm[:, c, :], True, True, [B(Qn), B("Pm" + sfx)], [pbuf[gp]])
            for c in range(NCHK):
                sl = slice(c * 64, (c + 1) * 64)
                mm(pbank[gq][0:64, sl], Pm[:, c, :], Qc[:, c, :], True, True, [B(Qn), B("Pm" + sfx)], [pbuf[gq]])
            if lvl < 4:
                cp("act", Pm, vw(pbank[gp][0:64, 0:NCHK * 64]), [pbuf[gp]], [B("Pm" + sfx)])
            cp("dve" if lvl % 2 == 0 else "act", Qm, vw(pbank[gq][0:64, 0:NCHK * 64]), [pbuf[gq]], [B("Qm" + sfx)])
            Qc, Qn = Qm, "Qm" + sfx
            gr = bank("g")
            for c in range(NCHK):
                sl = slice(c * 64, (c + 1) * 64)
                mm(pbank[gr][0:64, sl], Qm[:, c, :], Rm[:, c, :], True, True, [B("Qm" + sfx), B("Rm" + sfx)], [pbuf[gr]])
            tt("dve", Rm, vw(pbank[gr][0:64, 0:NCHK * 64]), Rm, ALU.add, [pbuf[gr], B("Rm" + sfx)], [B("Rm" + sfx)])
        tb = bank("t"); pT = pbank[tb].bitcast(BF)
        for c in range(NCHK):
            tr(pT[0:64, c * 128:(c + 1) * 128], knT[:, h, c * 64:(c + 1) * 64], identb, [kn_b, B("identb")], [pbuf[tb]])
        kview = vw(pT[0:64, 0:NCHK * 128], 128)
        tt("dve", kbg, kview, c_bg[:, :, h].unsqueeze(2).broadcast_to([64, NCHK, 128]), ALU.mult, [pbuf[tb], BS("c_bg")], [B("kbg")])
        tt("dve", kdm[h], kview, c_kd[:, :, h].unsqueeze(2).broadcast_to([64, NCHK, 128]), ALU.mult, [pbuf[tb], BS("c_kd")], [B("kdm%d" % (h % 4))])
        tb = bank("t"); pT = pbank[tb].bitcast(BF)
        for c in range(NCHK):
            tr(pT[0:64, c * 128:(c + 1) * 128], vT[:, h, c * 64:(c + 1) * 64], identb, [v_b, B("identb")], [pbuf[tb]])
        tt("dve", vbm, vw(pT[0:64, 0:NCHK * 128], 128), beta[:, :, h].unsqueeze(2).broadcast_to([64, NCHK, 128]), ALU.mult,
           [pbuf[tb], BS("beta")], [B("vbm")])
        gu = bank("g")
        for c in range(NCHK):
            mm(pbank[gu][0:64, c * 128:(c + 1) * 128], Rm[:, c, :], vbm[:, c, :], True, True, [B("Rm" + sfx), B("vbm")], [pbuf[gu]])
        cp("act", Um[h], vw(pbank[gu][0:64, 0:NCHK * 128], 128), [pbuf[gu]], [B("Um%d" % (h % 4))])
        gw = bank("g")
        for c in range(NCHK):
            mm(pbank[gw][:, c * 64:(c + 1) * 64], kbg[:, c, :], Rm[:, c, :], True, True, [B("kbg"), B("Rm" + sfx)], [pbuf[gw]])
        cp("act", WT[h], vw(pbank[gw][:, 0:NCHK * 64]), [pbuf[gw]], [B("WT%d" % (h % 4))])

    ocnt = [0]

    def chain_step(c, h, own):
        sc_raw = SC("sc_raw"); beta = SC("beta"); gtok = SC("gtok"); etmp = SC("etmp"); gc = SC("gc"); glb = SC("glb"); eglb = SC("eglb"); c_bg = SC("c_bg"); c_kd = SC("c_kd"); c_eg = SC("c_eg"); negb = SC("negb"); tmp88 = SC("tmp88"); vcm = SC("vcm")
        Sbb = B("Sb_%d" % h); S32b = B("S32_%d" % h)
        g1 = bank("g")
        mm(pbank[g1][0:64, 0:128], WT[h][:, c, :], Sb[:, h, :], True, True, [B("WT%d" % (h % 4)), Sbb], [pbuf[g1]])
        tt("dve", vnew[h], Um[h][:, c, :], pbank[g1][0:64, 0:128], ALU.subtract, [B("Um%d" % (h % 4)), pbuf[g1]], [B("vnew%d" % (h % 4))])
        if own:
            i = ocnt[0] % 2; ocnt[0] += 1
            sl = slice(c * 64, (c + 1) * 64)
            go = bank("g")
            mm(pbank[go][0:64, 0:128], qnT[:, h, sl], Sb[:, h, :], True, True, [B("qnT_%d" % h), Sbb], [pbuf[go]])
            gi = bank("g")
            mm(pbank[gi][0:64, 0:128], AT[h][:, c, :], vnew[h], True, True, [B("AT%d" % (h % 4)), B("vnew%d" % (h % 4))], [pbuf[gi]])
            act(o1[i], pbank[go][0:64, 0:128], AF.Identity, [pbuf[go], BS("c_eg")], [B("o1_%d" % i)], scale=c_eg[:, c, h:h + 1])
            tt("dve", o1[i], o1[i], pbank[gi][0:64, 0:128], ALU.add, [B("o1_%d" % i), pbuf[gi]], [B("o1_%d" % i)])
            memset("pool", oss[:, i:i + 1], 0.0, [B("oss%d" % i)])
            act(ojunk, o1[i], AF.Square, [B("o1_%d" % i), B("oss%d" % i)], [B("ojunk"), B("oss%d" % i)], accum=oss[:, i:i + 1])
            rsqrt(oss[:, i:i + 1], oss[:, i:i + 1], 1.0 / 128, EPS, [B("oss%d" % i)], [B("oss%d" % i)])
            ts("dve", onb[i], o1[i], oss[:, i:i + 1], None, ALU.mult, None, [B("o1_%d" % i), B("oss%d" % i)], [B("onb%d" % i)])
            tb = bank("t"); pT = pbank[tb].bitcast(BF)
            tr(pT[:, 0:64], onb[i], identb[0:64, 0:64], [B("onb%d" % i), B("identb")], [pbuf[tb]])
            stt("dve", oT[:, h, sl], pT[:, 0:64], ggain[:, 0:1], zT[:, h, sl], ALU.mult, ALU.mult,
                [pbuf[tb], B("ggain"), B("zT_%d" % h)], [B("oT_%d" % h)])
        g2 = bank("g")
        mm(pbank[g2][:, 0:128], kdm[h][:, c, :], vnew[h], True, True, [B("kdm%d" % (h % 4)), B("vnew%d" % (h % 4))], [pbuf[g2]])
        stt("dve", S32[:, h, :], S32[:, h, :], eglb[:, c, h:h + 1], pbank[g2][:, 0:128], ALU.mult, ALU.add,
            [S32b, BS("eglb"), pbuf[g2]], [S32b])
        cp("act", Sb[:, h, :], S32[:, h, :], [S32b], [Sbb])

    def swa_kv(t):
        cp("pool", kbT[:, :, 0:128], kbT[:, :, TT:TT + 128], [B("kbT")], [B("kbT")])
        cp("pool", vtok[:, 0, :], vtok[:, NBLK, :], [B("vtok")], [B("vtok")])
        cp("pool", keyneg[:, 0:1], keyneg[:, NBLK:NBLK + 1], [B("keyneg")], [B("keyneg")])
        for j in range(2):
            wb, wbl = wstream(wkb_d, j * 256)
            for r in range(2):
                pb = proj_fm(wb, wbl, r * 128)
                rmsnorm64(pb, kg8[:, 0:1], B("kg8"), kbT[:, 2 * j + r, 128:128 + TT], [B("kbT")])
        for j in range(2):
            wb, wbl = wstream(wvb_d, j * 256)
            for b in range(NBLK):
                pb = bank("p")
                for kc in range(16):
                    mm(pbank[pb][:, 0:256], hT[:, kc, b * 128:(b + 1) * 128], wb[:, kc, :], kc == 0, kc == 15,
                       [wbl[kc], hbs[kc]], [pbuf[pb]])
                cp("act", vtok[:, 1 + b, j * 256:(j + 1) * 256], pbank[pb][:, 0:256], [pbuf[pb]], [B("vtok")])
        for b in range(NBLK):
            r0 = t * TT + b * 128
            dma("sp", tbuf[:, b:b + 1], validc_d[r0:r0 + 128, :], [], [B("tbuf")])
            ts("dve", keyneg[:, 1 + b:2 + b], tbuf[:, b:b + 1], BIG, -BIG, ALU.mult, ALU.add, [B("tbuf")], [B("keyneg")])

    def own_proj(q_only=False):
        for p in range(4):
            wb, wbl = wstream(wq_d, p * 256)
            for r in range(2):
                h = 2 * p + r
                pb = proj_fm(wb, wbl, r * 128)
                conv_chunk(pb, 16 + h, qnT[:, h, :], [B("qnT_%d" % h)], True, scale_q=128.0 ** -0.5)
        if q_only:
            return
        for p in range(4):
            wb, wbl = wstream(wz_d, p * 256)
            for r in range(2):
                h = 2 * p + r
                pb = proj_fm(wb, wbl, r * 128)
                act(zT[:, h, :], pbank[pb][:, 0:TT], AF.Silu, [pbuf[pb]], [B("zT_%d" % h)])

    def own_proj_swa():
        for p in range(4):
            wb, wbl = wstream(wqb_d, p * 256)
            for hh in range(4):
                hd = 4 * p + hh
                pb = bank("p")
                for kc in range(16):
                    mm(pbank[pb][0:64, 0:TT], wb[:, kc, hh * 64:(hh + 1) * 64], hT[:, kc, :], kc == 0, kc == 15,
                       [wbl[kc], hbs[kc]], [pbuf[pb]])
                act(sqk[0:64, :], pbank[pb][0:64, 0:TT], AF.Square, [pbuf[pb]], [B("sqk")])
                g = bank("g")
                mm(pbank[g][0:64, 0:TT], blk64[0:64, 0:64], sqk[0:64, :], True, True, [B("blk64"), B("sqk")], [pbuf[g]])
                rsqrt(rinv[0:64, :], pbank[g][0:64, 0:TT], 1.0, 64.0 * EPS, [pbuf[g]], [B("rinv")])
                stt("dve", q2T64[:, hd, :], pbank[pb][0:64, 0:TT], qg8[0:64, 0:1], rinv[0:64, :], ALU.mult, ALU.mult,
                    [pbuf[pb], B("rinv"), B("qg8")], [B("knT_%d" % hd if hd < 8 else "vT_%d" % (hd - 8))])
        for p in range(4):
            wb, wbl = wstream(wzb_d, p * 256)
            for r in range(2):
                pb = proj_fm(wb, wbl, r * 128)
                act(zbT[:, 2 * p + r, :], pbank[pb][:, 0:TT], AF.Silu, [pbuf[pb]], [B("zT_%d" % (2 * p + r))])

    def swa_attn():
        for p in range(8):
            g = p // 2
            bi = p % 2
            dma("sp", biasb[bi], biasT_d[:, p * 512:(p + 1) * 512], [], [B("biasb0")])
            for b in range(NBLK):
                qs = slice(b * 128, (b + 1) * 128)
                gl = bank("g")
                for r in range(2):
                    R = slice(64 * r, 64 * r + 64)
                    for kb in range(2):
                        ks = slice((b + kb) * 128, (b + kb + 1) * 128)
                        mm(pbank[gl][:, (r * 2 + kb) * 128:(r * 2 + kb + 1) * 128], kbT[0:64, g, ks], q2T64[:, 2 * p + r, qs],
                           True, True, [B("kbT"), B("knT_%d" % (2 * p + r) if 2 * p + r < 8 else "vT_%d" % (2 * p + r - 8))], [pbuf[gl]])
                stt("dve", tbuf, pbank[gl][:, :], 0.125, biasb[bi], ALU.mult, ALU.add, [pbuf[gl], B("biasb0")], [B("tbuf")])
                t4 = tbuf.rearrange("p (r k q) -> p r k q", r=2, k=2)
                p4 = pTb.rearrange("p (r k q) -> p r k q", r=2, k=2)
                for r in range(2):
                    for kb in range(2):
                        ix = slice((r * 2 + kb) * 128, (r * 2 + kb + 1) * 128)
                        act(pTb[:, ix], tbuf[:, ix], AF.Exp, [B("tbuf"), B("keyneg")], [B("pTb")],
                            bias=keyneg[:, b + kb:b + kb + 1])
                g2 = bank("g")
                for r in range(2):
                    for part in range(2):
                        c0 = r * 256 + part * 128
                        for kb in range(2):
                            lhs = vtok[:, b + kb, g * 128:(g + 1) * 128] if part == 0 else onesb
                            mm(pbank[g2][:, c0:c0 + 128], lhs, p4[:, r, kb, :], kb == 0, kb == 1,
                               [B("vtok"), B("onesb"), B("pTb")], [pbuf[g2]])
                for r in range(2):
                    R = slice(64 * r, 64 * r + 64)
                    hd = 2 * p + r
                    ts("dve", rden[R, :], pbank[g2][R, r * 256 + 128:r * 256 + 256], esink[R, hd:hd + 1], None, ALU.add, None,
                       [pbuf[g2], B("esink")], [B("rden")])
                    recip(rden[R, :], rden[R, :], [B("rden")], [B("rden")])
                    tt("dve", otmp[R, :], pbank[g2][R, r * 256:r * 256 + 128], rden[R, :], ALU.mult, [pbuf[g2], B("rden")], [B("otmp")])
                    tt("pool", obT[R, p, qs], otmp[R, :], zbT[R, p, qs], ALU.mult, [B("otmp"), B("zT_%d" % p)], [B("qnT_%d" % p)])

    wbb_i = [0]

    def final(t):
        oi = t - FIRST_OWN
        for cc in range(16):
            wb, wbl = wstream(wgab_d, cc * 256)
            i = wbb_i[0] % 2; wbb_i[0] += 1
            dma("pool", wbb[i], wbr_d[:, cc * 256:(cc + 1) * 256].rearrange("(k p) c -> p k c", p=128), [], wbb_bufs[i])
            pa = proj_fm(wb, wbl, 0)
            pbb = proj_fm(wb, wbl, 128)
            gy = bank("g")
            for h in range(8):
                mm(pbank[gy][:, 0:TT], wbb[i][:, h, 0:128], oT[:, h, :], h == 0, h == 7, wbb_bufs[i] + [B("oT_%d" % h)], [pbuf[gy]])
            gs = bank("g")
            for p in range(8):
                mm(pbank[gs][:, 0:TT], wbb[i][:, p, 128:256], obT[:, p, :], p == 0, p == 7, wbb_bufs[i] + [B("qnT_%d" % p)], [pbuf[gs]])
            act(sga, pbank[pa][:, 0:TT], AF.Sigmoid, [pbuf[pa]], [B("cacc")])
            act(sgb, pbank[pbb][:, 0:TT], AF.Sigmoid, [pbuf[pbb]], [B("cact2")])
            tt("dve", mt1, sga, pbank[gy][:, 0:TT], ALU.mult, [B("cacc"), pbuf[gy]], [B("rinv")])
            tt("dve", mt2, sgb, pbank[gs][:, 0:TT], ALU.mult, [B("cact2"), pbuf[gs]], [B("pre0")])
            tt("pool", mixedT[:, cc, :], mt1, mt2, ALU.add, [B("rinv"), B("pre0")], [B("mx_%d" % cc), B("xn%d" % (cc * TT // D))])
        ri = 0
        for cb in range(8):
            wb, wbl = wstream(wout_d, cb * 256)
            cs = slice(cb * 256, (cb + 1) * 256)
            for s in range(NBLK):
                pb = bank("p")
                for kc in range(16):
                    mm(pbank[pb][:, 0:256], mixedT[:, kc, s * 128:(s + 1) * 128], wb[:, kc, :], kc == 0, kc == 15,
                       [wbl[kc], B("mx_%d" % kc), B("xn%d" % (kc // 8))], [pbuf[pb]])
                i = ri % 2; ri += 1
                tt("dve", res[i], pbank[pb][:, 0:256], gate_row[:, cs], ALU.mult, [pbuf[pb], B("gate_row")], [B("tbuf")])
                tt("pool", xs[s][:, cs], xs[s][:, cs], res[i], ALU.add, [B("xs%d" % s), B("tbuf")], [B("xs%d" % s)])
        for s in range(NBLK):
            r0 = oi * TT + s * 128
            dma("sp", out_d[r0:r0 + 128, :], xs[s], [B("xs%d" % s)], [B("out_dram")])

    def S32M(h):
        return zT[:, h, :].bitcast(F32)

    def SbM(h):
        return oT[:, h, 0:128]

    def chain_step_M(c, h):
        sc_raw = SC("sc_raw"); beta = SC("beta"); gtok = SC("gtok"); etmp = SC("etmp"); gc = SC("gc"); glb = SC("glb"); eglb = SC("eglb"); c_bg = SC("c_bg"); c_kd = SC("c_kd"); c_eg = SC("c_eg"); negb = SC("negb"); tmp88 = SC("tmp88"); vcm = SC("vcm")
        zb_, ob_ = B("zT_%d" % h), B("oT_%d" % h)
        g1 = bank("g")
        mm(pbank[g1][0:64, 0:128], WT[h][:, c, :], SbM(h), True, True, [B("WT%d" % (h % 4)), ob_], [pbuf[g1]])
        ts("dve", vnewM[h], pbank[g1][0:64, 0:128], -1.0, None, ALU.mult, None, [pbuf[g1]], [B("vnewM%d" % (h % 4))])
        g2 = bank("g")
        mm(pbank[g2][:, 0:128], kdm[h][:, c, :], vnewM[h], True, True, [B("kdm%d" % (h % 4)), B("vnewM%d" % (h % 4))], [pbuf[g2]])
        stt("dve", S32M(h), S32M(h), eglb[:, c, h:h + 1], pbank[g2][:, 0:128], ALU.mult, ALU.add,
            [zb_, BS("eglb"), pbuf[g2]], [zb_])
        cp("act", SbM(h), S32M(h), [zb_], [ob_])

    def exchange_and_fold():
        k = 0
        for j in range(8):
            for h in range(8):
                st = tbuf[:, (k % 2) * 256:(k % 2) * 256 + 256]; k += 1
                ts("dve", st[:, 0:128], S32[:, h, :], selr[:, j:j + 1], None, ALU.mult, None,
                   [B("S32_%d" % h), B("selr"), B("tbuf")], [B("tbuf")])
                ts("dve", st[:, 128:256], S32M(h), selr[:, j:j + 1], None, ALU.mult, None,
                   [B("zT_%d" % h), B("selr"), B("tbuf")], [B("tbuf")])
                dma("sp", sum_src.ap()[j * 128:(j + 1) * 128, h * 256:(h + 1) * 256], st, [B("tbuf")], [B("sum_src")])
        P.op("pool", lambda e: e.collective_compute("AllReduce", ALU.add, replica_groups=[list(range(NCORES))],
                                                    ins=[sum_src.ap().opt()], outs=[sum_dst.ap().opt()]),
             reads=[B("sum_src")], writes=[B("sum_dst")], dma=True, cost=300.0, busy=5.0, inc=1)
        for h in range(8):
            memset("pool", S32[:, h, :], 0.0, [B("S32_%d" % h)])
            memset("pool", Sb[:, h, :], 0.0, [B("Sb_%d" % h)])
            for j in range(7):
                blk = tbuf[:, 0:256]
                dma("sp", blk, sum_dst.ap()[j * 128:(j + 1) * 128, h * 256:(h + 1) * 256], [B("sum_dst")], [B("tbuf")])
                gt = bank("g")
                P.op("pe", lambda e, o=pbank[gt][:, 0:128], i_=blk[:, 128:256]: e.transpose(o, i_, identf),
                     reads=[B("tbuf"), B("identf")], writes=[pbuf[gt]], cost=0.6, busy=0.3)
                cp("act", pTb[:, 0:128], pbank[gt][:, 0:128], [pbuf[gt]], [B("pTb")])
                g2 = bank("g")
                mm(pbank[g2][:, 0:128], pTb[:, 0:128], Sb[:, h, :], True, True, [B("pTb"), B("Sb_%d" % h)], [pbuf[g2]])
                tt("dve", otmp, pbank[g2][:, 0:128], blk[:, 0:128], ALU.add, [pbuf[g2], B("tbuf")], [B("otmp")])
                stt("dve", S32[:, h, :], otmp, selr[:, j + 1:j + 2], S32[:, h, :], ALU.mult, ALU.add,
                    [B("otmp"), B("selr"), B("S32_%d" % h)], [B("S32_%d" % h)])
                cp("act", Sb[:, h, :], otmp, [B("otmp")], [B("Sb_%d" % h)])
            cp("act", Sb[:, h, :], S32[:, h, :], [B("S32_%d" % h)], [B("Sb_%d" % h)])

    class Stop(Exception):
        pass

    def kv_path():
        for h in range(8):
            pb = proj_fm(wk, wkbl, h * 128)
            conv_chunk(pb, h, knT[:, h, :], [B("knT_%d" % h)], True)
        for h in range(8):
            pb = proj_fm(wv, wvbl, h * 128)
            conv_chunk(pb, 8 + h, vT[:, h, :], [B("vT_%d" % h)], False)

    def main_loop():
        for h in range(8):
            cp("dve", S32M(h), identf, [B("identf")], [B("zT_%d" % h)])
            cp("act", SbM(h), identf, [B("identf")], [B("oT_%d" % h)])
        for t in range(NT):
            cur_tp[0] = t % 2
            load_h(t)
            kv_path()
            if t < FIRST_OWN:
                continue
            scalars(t)
            for hg in range(2):
                for h in range(4 * hg, 4 * hg + 4):
                    head_state(h, False)
                for c in range(NCHK):
                    for h in range(4 * hg, 4 * hg + 4):
                        chain_step(c, h, False)
                        chain_step_M(c, h)
        if stop == 1: raise Stop()
        exchange_and_fold()
        if stop == 2: raise Stop()
        memset("pool", halo, 0.0, [B("halo%d" % c) for c in range(24)])
        for t in range(NT):
            own = t >= FIRST_OWN
            cur_tp[0] = t % 2
            load_h(t)
            kv_path()
            if t >= FIRST_OWN - 1:
                swa_kv(t)
            if not own:
                if t == FIRST_OWN - 1:
                    own_proj(q_only=True)
                continue
            scalars(t)
            own_proj()
            for hg in range(2):
                for h in range(4 * hg, 4 * hg + 4):
                    head_state(h, True)
                for c in range(NCHK):
                    for h in range(4 * hg, 4 * hg + 4):
                        chain_step(c, h, True)
            own_proj_swa()
            swa_attn()
            final(t)
        if debug:
            dma("sp", dbg["S"], S32.rearrange("p k t -> p (k t)"), [B("S32_%d" % h) for h in range(8)], [B("dbg_S")])

    def main_loop_replay():
      for t in range(NT):
        own = t >= FIRST_OWN
        if stop == 0: raise Stop()
        cur_tp[0] = t % 2
        load_h(t)
        if stop == 1: raise Stop()
        for h in range(8):
            pb = proj_fm(wk, wkbl, h * 128)
            conv_chunk(pb, h, knT[:, h, :], [B("knT_%d" % h)], True)
        for h in range(8):
            pb = proj_fm(wv, wvbl, h * 128)
            conv_chunk(pb, 8 + h, vT[:, h, :], [B("vT_%d" % h)], False)
        if stop == 2: raise Stop()
        scalars(t)
        if stop == 3: raise Stop()
        if t >= FIRST_OWN - 1:
            swa_kv(t)
        if own:
            own_proj()
        elif t == FIRST_OWN - 1:
            own_proj(q_only=True)
        for hg in range(2):
            for h in range(4 * hg, 4 * hg + 4):
                head_state(h, own)
            if stop == 4: raise Stop()
            for c in range(NCHK):
                for h in range(4 * hg, 4 * hg + 4):
                    chain_step(c, h, own)
        if stop == 5 + (1 if own else 0): raise Stop()
        if own:
            own_proj_swa()
            swa_attn()
            if stop == 7: raise Stop()
            final(t)
        if debug and t == NT - 1:
            dma("sp", dbg["hT"], hT.rearrange("p k t -> p (k t)"), hbs, [B("dbg_hT")])
            dma("sp", dbg["knT"], knT.rearrange("p k t -> p (k t)"), [B("knT_%d" % h) for h in range(8)], [B("dbg_knT")])
            dma("sp", dbg["vT"], vT.rearrange("p k t -> p (k t)"), [B("vT_%d" % h) for h in range(8)], [B("dbg_vT")])
            dma("sp", dbg["S"], S32.rearrange("p k t -> p (k t)"), [B("S32_%d" % h) for h in range(8)], [B("dbg_S")])
            dma("sp", dbg["oT"], oT.rearrange("p k t -> p (k t)"), [B("oT_%d" % h) for h in range(8)], [B("dbg_oT")])
            dma("sp", dbg["obT"], obT.rearrange("p k t -> p (k t)"), [B("qnT_%d" % h) for h in range(8)], [B("dbg_obT")])
            dma("sp", dbg["gc"], gc.rearrange("p c h -> p (c h)"), [B("gc")], [B("dbg_gc")])
            dma("sp", dbg["beta"], beta.rearrange("p c h -> p (c h)"), [B("beta")], [B("dbg_beta")])
            dma("sp", dbg["modP"], modP[:, 0:32], [B("modP")], [B("dbg_modP")])
            dma("sp", dbg["gate_row"], gate_row, [B("gate_row")], [B("dbg_gate_row")])
            dma("sp", dbg["cact"], cact, [B("cact")], [B("dbg_cact")])

    try:
        if MODE == "cc":
            main_loop()
        else:
            main_loop_replay()
    except Stop:
        pass
    P.emit(stack)
    stack.close()
    return nc


def _t5_bucket(dist):
    max_exact = 16
    d = np.maximum(dist, 1).astype(np.float32)
    large = max_exact + (np.log(d / max_exact) / np.log(128 / max_exact) * (32 - max_exact)).astype(np.int32)
    large = np.minimum(large, 31)
    return np.where(dist < max_exact, dist, large)


def prep_common(inp):
    f = np.float32
    l = 0
    w_in = np.asarray(inp["w_in"][l], f)
    offs = np.cumsum([0, 1024, 1024, 1024, 1024, 8, 8, 1024, 256, 256, 1024, 2048, 2048])
    seg = lambda i: w_in[:, offs[i]:offs[i + 1]]
    q_a, k_a, v_a, z_a, b_a, a_a, q_b, k_b, v_b, z_b, gl_a, gl_b = [seg(i) for i in range(12)]
    P = lambda v, n: np.ascontiguousarray(np.asarray(v, f).reshape(n, 128).T)
    rep = lambda v: np.ascontiguousarray(np.broadcast_to(np.asarray(v, f)[None, :], (128, len(v))))
    dup = lambda w: np.concatenate([np.concatenate([w[:, g * 64:(g + 1) * 64]] * 2, axis=1) for g in range(4)], axis=1)
    wgab = np.concatenate([np.concatenate([gl_a[:, c * 128:(c + 1) * 128], gl_b[:, c * 128:(c + 1) * 128]], axis=1)
                           for c in range(16)], axis=1)
    wg = np.asarray(inp["w_branch_gdn"][l], f); ws = np.asarray(inp["w_branch_swa"][l], f)
    wbr = np.concatenate([np.concatenate([wg[:, c * 128:(c + 1) * 128], ws[:, c * 128:(c + 1) * 128]], axis=1)
                          for c in range(16)], axis=1)
    conv = np.asarray(inp["conv_w"][l], f)
    conv_r = conv.reshape(4, 3, 8, 128)
    convw = np.stack([conv_r[:, 1], conv_r[:, 2], conv_r[:, 0]], axis=1).reshape(4, 24, 128).transpose(2, 1, 0)
    b_ada = np.asarray(inp["b_ada"][l], f)
    kk = np.arange(128)[:, None, None]; kb = np.arange(2)[None, :, None]; q = np.arange(128)[None, None, :]
    dist = 128 + q - kb * 128 - kk
    ok = (dist >= 0) & (dist < 128)
    rb = np.asarray(inp["rel_bias"], f)
    gathered = rb[_t5_bucket(np.maximum(dist, 0))]
    biasT = np.where(ok[..., None], gathered, f(-BIG)).transpose(0, 3, 1, 2)
    ii = np.arange(64)
    com = {
        "cT": P(inp["c"][0], 16), "w_ada": np.ascontiguousarray(inp["w_ada"][l], f),
        "badaP": P(b_ada, 48), "bgate_rep": rep(b_ada[4096:]), "bada_rep": rep(b_ada[:4096]), "gainP": P(inp["norm_gain"][l], 16),
        "wk": np.ascontiguousarray(k_a), "wv": np.ascontiguousarray(v_a),
        "wbg": np.ascontiguousarray(np.concatenate([b_a, a_a], axis=1)),
        "wq": np.ascontiguousarray(q_a), "wz": np.ascontiguousarray(z_a), "wqb": np.ascontiguousarray(q_b),
        "wkb_dup": np.ascontiguousarray(dup(k_b)), "wvb_dup": np.ascontiguousarray(dup(v_b)),
        "wzb": np.ascontiguousarray(z_b), "wgab": np.ascontiguousarray(wgab), "wbr": np.ascontiguousarray(wbr),
        "wout": np.ascontiguousarray(inp["w_out"][l], f),
        "convw": np.ascontiguousarray(convw), "alog_rep": rep(inp["a_log"][l]), "dtb_rep": rep(inp["dt_bias"][l]),
        "gdn_gainP": np.ascontiguousarray(np.asarray(inp["gdn_norm_gain"][l], f).reshape(128, 1)),
        "qgain2": np.ascontiguousarray(np.tile(np.asarray(inp["q_norm_gain"][l], f), 2).reshape(128, 1)),
        "kgain2": np.ascontiguousarray(np.tile(np.asarray(inp["k_norm_gain"][l], f), 2).reshape(128, 1)),
        "sinks_rep": rep(inp["sinks"][l]), "biasT": np.ascontiguousarray(biasT.reshape(128, -1), f),
        "identf": np.eye(128, dtype=f), "tri": np.triu(np.ones((128, 128), f)),
        "blk64": np.kron(np.eye(2, dtype=f), np.ones((64, 64), f)),
        "maskL": np.where(ii[None, :] < ii[:, None], f(0), f(BIG)).astype(f),
        "maskU": np.where(ii[None, :] >= ii[:, None], f(0), f(-BIG)).astype(f),
    }
    return com


def _validm(v, NT):
    return np.ascontiguousarray(v.reshape(NT, NCHK, 64).transpose(0, 2, 1).reshape(NT * 64, NCHK))


def prep_core(x2d, core, NT, NOWN, ncores):
    NTOK = NT * TT
    nown = NOWN * TT
    s0 = core * nown
    xp = np.zeros((NTOK, D), np.float32)
    v = np.zeros((NTOK,), np.float32)
    xp[NTOK - nown:] = x2d[s0:s0 + nown]
    v[NTOK - nown:] = 1.0
    nh = NTOK - nown
    if core > 0:
        xp[:nh] = x2d[s0 - nh:s0]
        v[:nh] = 1.0
    sel = np.zeros((128, 8), np.float32); sel[:, core] = 1.0
    return {"x": xp, "valid": np.ascontiguousarray(np.broadcast_to(v[None, :], (128, NTOK))),
            "validc": np.ascontiguousarray(v.reshape(NTOK, 1)), "sel_rep": sel, "validm": _validm(v, NT)}


def prep_core_replay(x2d, core, NT, NOWN, ncores):
    NTOK = NT * TT
    nreal = (core + 1) * NOWN * TT
    xp = np.zeros((NTOK, D), np.float32)
    xp[NTOK - nreal:] = x2d[:nreal]
    v = np.zeros((NTOK,), np.float32); v[NTOK - nreal:] = 1.0
    return {"x": xp, "valid": np.ascontiguousarray(np.broadcast_to(v[None, :], (128, NTOK))),
            "validc": np.ascontiguousarray(v.reshape(NTOK, 1)), "validm": _validm(v, NT)}


_CACHE = {}


def kernel(**inputs):
    x = np.asarray(inputs["x"], np.float32)
    T = x.shape[1]
    NOWN = T // NCORES // TT
    NT = NOWN + 1 if MODE == "cc" else T // TT
    if "nc" not in _CACHE:
        _CACHE["nc"] = build(NT, NOWN)
    nc = _CACHE["nc"]
    com = prep_common(inputs)
    in_maps = []
    for c in range(NCORES):
        m = dict(com)
        if MODE == "cc":
            m.update(prep_core(x[0], c, NT, NOWN, NCORES))
        else:
            m.update(prep_core_replay(x[0], c, NT, NOWN, NCORES))
            m["sel_rep"] = np.zeros((128, 8), np.float32)
        in_maps.append(m)
    res = run_bass_kernel_spmd(nc, in_maps, core_ids=list(range(NCORES)))
    out = np.concatenate([np.asarray(res.results[c]["out"], np.float32) for c in range(NCORES)], axis=0)
    return out.reshape(1, T, D)
```
